# Optimizing a Trainium2 kernel written in Bass

```python
import math
import jax, jax.numpy as jnp
from jax import lax
import numpy as np

D_MODEL = 1024
BATCH = 4
SEQ = 4096
DEPTH = 4
DEC_BATCH = 128
DEC_SEQ = 8
PAST_LEN = 8192
PAGE_SIZE = 128

CHUNK = 128
A_WIDTH = D_MODEL
A_GROUPS = 8
A_GDIM = A_WIDTH // A_GROUPS
N_HEADS = 16
HEAD_DIM = 64
N_KV_HEADS = 4
Q_PER_KV = N_HEADS // N_KV_HEADS
WINDOW = 128
N_BUCKETS = 32
MAX_DISTANCE = 128
D_FF = -(-8 * D_MODEL // (3 * 256)) * 256
DEEPNORM_ALPHA = (2 * DEPTH) ** 0.25
DEEPNORM_BETA = (8 * DEPTH) ** -0.25
LN_EPS = 1e-5
NEG_INF = -1e30

O_U = 0
O_V = O_U + A_WIDTH
O_Q = O_V + A_WIDTH
O_K = O_Q + N_HEADS * HEAD_DIM
O_VV = O_K + N_KV_HEADS * HEAD_DIM
O_G = O_VV + N_KV_HEADS * HEAD_DIM
IN_WIDTH = O_G + 2 * D_MODEL

kernel_name = "hybrid_gmlp_swa_sink_decoder_step"


def layer_norm(x, g, b):
    xf = x.astype(jnp.float32)
    mu = jnp.mean(xf, axis=-1, keepdims=True)
    var = jnp.mean(jnp.square(xf - mu), axis=-1, keepdims=True)
    return ((xf - mu) * lax.rsqrt(var + LN_EPS) * g.astype(jnp.float32) + b.astype(jnp.float32)).astype(x.dtype)


def rel_bucket(dist):
    n = jnp.maximum(dist, 0)
    max_exact = N_BUCKETS // 2
    nf = jnp.maximum(n, 1).astype(jnp.float32)
    large = max_exact + (jnp.log(nf / max_exact) / math.log(MAX_DISTANCE / max_exact)
                         * (N_BUCKETS - max_exact)).astype(jnp.int32)
    large = jnp.minimum(large, N_BUCKETS - 1)
    return jnp.where(n < max_exact, n, large)


def rel_bias_blocks(rel_bias, dist):
    b = rel_bias[rel_bucket(dist)].astype(jnp.float32)
    b = jnp.transpose(b, (2, 0, 1))
    return b.reshape(N_KV_HEADS, Q_PER_KV, dist.shape[0], dist.shape[1])


def sink_attention(q, k, v, bias, mask, sinks):
    s = jnp.einsum('bnqgrd,bnkgd->bngrqk', q, k).astype(jnp.float32) * (HEAD_DIM ** -0.5) + bias
    s = jnp.where(mask[None, :, None, None], s, NEG_INF)
    sink = sinks.astype(jnp.float32).reshape(N_KV_HEADS, Q_PER_KV)[None, None, :, :, None, None]
    m = jnp.maximum(jnp.max(s, axis=-1, keepdims=True), sink)
    p = jnp.exp(s - m)
    p = p / (jnp.sum(p, axis=-1, keepdims=True) + jnp.exp(sink - m))
    return jnp.einsum('bngrqk,bnkgd->bnqgrd', p.astype(v.dtype), v)


def causal_chunk_mask():
    return jnp.tril(jnp.ones((CHUNK, CHUNK), dtype=bool))


def sgu_prompt(u, v, w_s, b_s):
    b, s, _ = v.shape
    vv = v.reshape(b, s // CHUNK, CHUNK, A_GROUPS, A_GDIM)
    w = jnp.where(causal_chunk_mask()[None], w_s, 0.0).astype(v.dtype)
    mixed = jnp.einsum('gts,bcsgd->bctgd', w, vv) + b_s.T.astype(v.dtype)[None, None, :, :, None]
    return u * mixed.reshape(b, s, A_WIDTH)


def sgu_sample(u, v, w_s, b_s):
    b, s, _ = v.shape
    vv = v.reshape(b, s, A_GROUPS, A_GDIM)
    w = jnp.where(causal_chunk_mask()[None], w_s, 0.0)[:, :s, :s].astype(v.dtype)
    mixed = jnp.einsum('gts,bsgd->btgd', w, vv) + b_s[:, :s].T.astype(v.dtype)[None, :, :, None]
    return u * mixed.reshape(b, s, A_WIDTH)


def swa_prompt(q, k, v, bias, mask, sinks):
    b, s = q.shape[0], q.shape[1]
    nb = s // WINDOW
    pad = jnp.zeros((b, WINDOW, N_KV_HEADS, HEAD_DIM), k.dtype)
    kp = jnp.concatenate([pad, k], axis=1)[:, :s]
    vp = jnp.concatenate([pad, v], axis=1)[:, :s]
    blk = lambda t: t.reshape(b, nb, WINDOW, N_KV_HEADS, HEAD_DIM)
    kk = jnp.concatenate([blk(kp), blk(k)], axis=2)
    vv = jnp.concatenate([blk(vp), blk(v)], axis=2)
    qb = q.reshape(b, nb, WINDOW, N_KV_HEADS, Q_PER_KV, HEAD_DIM)
    o = sink_attention(qb, kk, vv, bias, mask, sinks)
    return o.reshape(b, s, N_HEADS * HEAD_DIM)


def swa_sample(q, k_all, v_all, bias, mask, sinks):
    o = sink_attention(q[:, None], k_all[:, None], v_all[:, None], bias, mask, sinks)
    return o[:, 0].reshape(q.shape[0], q.shape[1], N_HEADS * HEAD_DIM)


def trunk_layer(x, w_in, ln_v_g, ln_v_b, w_s, b_s, sinks, w_pa, w_pb, w_o,
                ln1_g, ln1_b, w_gate, w_up, w_down, ln2_g, ln2_b,
                bias, mask, k_buf, v_buf):
    b, s, _ = x.shape
    h = x @ w_in
    u = jax.nn.gelu(h[..., O_U:O_V])
    va = layer_norm(jax.nn.gelu(h[..., O_V:O_Q]), ln_v_g, ln_v_b)
    q = h[..., O_Q:O_K].reshape(b, s, N_KV_HEADS, Q_PER_KV, HEAD_DIM)
    k = h[..., O_K:O_VV].reshape(b, s, N_KV_HEADS, HEAD_DIM)
    v = h[..., O_VV:O_G].reshape(b, s, N_KV_HEADS, HEAD_DIM)
    g_a = jax.nn.sigmoid(h[..., O_G:O_G + D_MODEL])
    g_b = jax.nn.sigmoid(h[..., O_G + D_MODEL:])
    if k_buf is None:
        y_a = sgu_prompt(u, va, w_s, b_s)
        y_b = swa_prompt(q, k, v, bias, mask, sinks)
        new_k, new_v = k[:, -WINDOW:], v[:, -WINDOW:]
    else:
        y_a = sgu_sample(u, va, w_s, b_s)
        k_all = jnp.concatenate([k_buf, k], axis=1)
        v_all = jnp.concatenate([v_buf, v], axis=1)
        y_b = swa_sample(q, k_all, v_all, bias, mask, sinks)
        new_k, new_v = k_all[:, -WINDOW:], v_all[:, -WINDOW:]
    mix = (g_a * (y_a @ w_pa) + g_b * (y_b @ w_pb)) @ w_o
    x = layer_norm(DEEPNORM_ALPHA * x + mix, ln1_g, ln1_b)
    ffn = (jax.nn.silu(x @ w_gate) * (x @ w_up)) @ w_down
    x = layer_norm(DEEPNORM_ALPHA * x + ffn, ln2_g, ln2_b)
    return x, new_k, new_v, va


def setup_inputs(seed: int = 0) -> dict:
    key = jax.random.key(seed)
    ks = jax.random.split(key, 24)
    f32 = jnp.float32
    nrm = lambda k, shp, sc: jax.random.normal(k, shp, f32) * sc
    return {
        "x_prompt": nrm(ks[0], (BATCH, SEQ, D_MODEL), 1.0),
        "x_sample": nrm(ks[1], (DEC_BATCH, DEC_SEQ, D_MODEL), 1.0),
        "cache_swa_k": nrm(ks[2], (DEPTH, DEC_BATCH, WINDOW, N_KV_HEADS, HEAD_DIM), 1.0),
        "cache_swa_v": nrm(ks[3], (DEPTH, DEC_BATCH, WINDOW, N_KV_HEADS, HEAD_DIM), 1.0),
        "rel_bias": nrm(ks[4], (N_BUCKETS, N_HEADS), 0.5),
        "w_in": nrm(ks[5], (DEPTH, D_MODEL, IN_WIDTH), D_MODEL ** -0.5),
        "ln_v_g": 1.0 + nrm(ks[6], (DEPTH, A_WIDTH), 0.05),
        "ln_v_b": nrm(ks[7], (DEPTH, A_WIDTH), 0.02),
        "w_s": nrm(ks[8], (DEPTH, A_GROUPS, CHUNK, CHUNK), CHUNK ** -0.5),
        "b_s": 1.0 + nrm(ks[9], (DEPTH, A_GROUPS, CHUNK), 0.1),
        "sinks": nrm(ks[10], (DEPTH, N_HEADS), 0.5),
        "w_pa": nrm(ks[11], (DEPTH, A_WIDTH, D_MODEL), A_WIDTH ** -0.5),
        "w_pb": nrm(ks[12], (DEPTH, N_HEADS * HEAD_DIM, D_MODEL), (N_HEADS * HEAD_DIM) ** -0.5),
        "w_o": nrm(ks[13], (DEPTH, D_MODEL, D_MODEL), DEEPNORM_BETA * D_MODEL ** -0.5),
        "ln1_g": 1.0 + nrm(ks[14], (DEPTH, D_MODEL), 0.05),
        "ln1_b": nrm(ks[15], (DEPTH, D_MODEL), 0.02),
        "w_gate": nrm(ks[16], (DEPTH, D_MODEL, D_FF), D_MODEL ** -0.5),
        "w_up": nrm(ks[17], (DEPTH, D_MODEL, D_FF), D_MODEL ** -0.5),
        "w_down": nrm(ks[18], (DEPTH, D_FF, D_MODEL), DEEPNORM_BETA * D_FF ** -0.5),
        "ln2_g": 1.0 + nrm(ks[19], (DEPTH, D_MODEL), 0.05),
        "ln2_b": nrm(ks[20], (DEPTH, D_MODEL), 0.02),
    }


def reference(x_prompt, x_sample, cache_swa_k, cache_swa_v, rel_bias, w_in, ln_v_g, ln_v_b,
              w_s, b_s, sinks, w_pa, w_pb, w_o, ln1_g, ln1_b, w_gate, w_up, w_down,
              ln2_g, ln2_b):
    n_blocks = x_prompt.shape[1] // WINDOW
    qi = jnp.arange(WINDOW, dtype=jnp.int32)[:, None]
    kj = jnp.arange(2 * WINDOW, dtype=jnp.int32)[None, :]
    dist_p = qi + WINDOW - kj
    bias_p = rel_bias_blocks(rel_bias, dist_p)
    blk = jnp.arange(n_blocks, dtype=jnp.int32)[:, None, None]
    mask_p = (dist_p >= 0)[None] & (dist_p < WINDOW)[None] & ((blk - 1) * WINDOW + kj[None] >= 0)
    n_new = x_sample.shape[1]
    qs = jnp.arange(n_new, dtype=jnp.int32)[:, None]
    ksj = jnp.arange(WINDOW + n_new, dtype=jnp.int32)[None, :]
    dist_s = qs + WINDOW - ksj
    bias_s = rel_bias_blocks(rel_bias, dist_s)
    mask_s = ((dist_s >= 0) & (dist_s < WINDOW))[None]

    xp, xs = x_prompt, x_sample
    kp_l, vp_l, ks_l, vs_l, ga_l = [], [], [], [], []
    for l in range(DEPTH):
        w = (w_in[l], ln_v_g[l], ln_v_b[l], w_s[l], b_s[l], sinks[l], w_pa[l], w_pb[l], w_o[l],
             ln1_g[l], ln1_b[l], w_gate[l], w_up[l], w_down[l], ln2_g[l], ln2_b[l])
        xp, kp, vp, _ = trunk_layer(xp, *w, bias_p, mask_p, None, None)
        xs, kn, vn, va_s = trunk_layer(xs, *w, bias_s, mask_s, cache_swa_k[l], cache_swa_v[l])
        kp_l.append(kp); vp_l.append(vp); ks_l.append(kn); vs_l.append(vn); ga_l.append(va_s)
    swa_k_prompt = jnp.stack(kp_l)
    swa_v_prompt = jnp.stack(vp_l)
    swa_k_sample = jnp.stack(ks_l)
    swa_v_sample = jnp.stack(vs_l)
    gmlp_v_sample = jnp.stack(ga_l)
    return (xp, xs, swa_k_prompt, swa_v_prompt, swa_k_sample, swa_v_sample, gmlp_v_sample)
```

```python
import math
import os
import numpy as np
import concourse.bass as bass
import concourse.mybir as mybir
from concourse.bass_utils import run_bass_kernel_spmd

F32 = mybir.dt.float32
BF16 = mybir.dt.bfloat16
I32 = mybir.dt.int32
AF = mybir.ActivationFunctionType
ALU = mybir.AluOpType
AX = mybir.AxisListType

D = 1024
DEPTH = 4
NCORE = 8
NT = 21
GT = 7
NG = 3
T = GT * 128
DFF = 2816
NHC = DFF // 128
INW = 5632
O_U, O_V, O_Q, O_K, O_G = 0, 1024, 2048, 3072, 3584
ALPHA = (2 * DEPTH) ** 0.25
LN_EPS = 1e-5
NEG = -1e30
RING = 6
BW = 256
SAMPLE_T = 20
LASTP_T = 19
FIRST_T = 4


class Eng:
    def __init__(self, name, h, sem):
        self.name, self.h, self.sem = name, h, sem
        self.cnt = 0
        self.waited = {}


class DmaQ:
    def __init__(self, eng, sems, tag):
        self.eng, self.sems, self.tag = eng, sems, tag
        self.k = 0


class Trk:
    def __init__(self):
        self.lw = {}
        self.rd = {}

    def deps(self, reads, writes, extra=()):
        d = {}

        def add(ev):
            if ev is None:
                return
            sem, val, key = ev
            if d.get(key, (None, 0))[1] < val:
                d[key] = (sem, val)

        for k in reads:
            add(self.lw.get(k))
        for k in writes:
            add(self.lw.get(k))
            for ev in self.rd.get(k, {}).values():
                add(ev)
        for ev in extra:
            add(ev)
        return d

    def wait(self, eng, d):
        for key, (sem, val) in d.items():
            if eng.waited.get(key, 0) < val:
                eng.h.wait_ge(sem, val)
                eng.waited[key] = val

    def commit(self, ev, reads, writes):
        for k in reads:
            if k.startswith("c:"):
                continue
            r = self.rd.setdefault(k, {})
            if r.get(ev[2], (None, 0, None))[1] < ev[1]:
                r[ev[2]] = ev
        for k in writes:
            self.lw[k] = ev
            self.rd[k] = {}


def build_program(groups=(0, 1, 2), depth_run=DEPTH):
    nc = bass.Bass("TRN2", target_bir_lowering=False)
    dram_in = {}

    def din(name, shape, dt=F32):
        dram_in[name] = nc.dram_tensor(name, list(shape), dt, kind="ExternalInput").ap()
        return dram_in[name]

    def dout(name, shape):
        return nc.dram_tensor(name, list(shape), F32, kind="ExternalOutput").ap()

    xin = din("xin", [NT * 128, D])
    ck = din("ck", [DEPTH, 16, 128, 256])
    cv = din("cv", [DEPTH, 16, 128, 256])
    w_in = din("w_in", [DEPTH, D, INW])
    w_pa = din("w_pa", [DEPTH, D, D])
    w_pb = din("w_pb", [DEPTH, D, D])
    w_o = din("w_o", [DEPTH, D, D])
    w_gate = din("w_gate", [DEPTH, D, DFF])
    w_up = din("w_up", [DEPTH, D, DFF])
    w_down = din("w_down", [DEPTH, DFF, D])
    lnp = din("lnp", [DEPTH, 6, D])
    wsT_p = din("wsT_p", [DEPTH, 128, 8, 128])
    wsT_s = din("wsT_s", [DEPTH, 128, 8, 128])
    bs_p = din("bs_p", [DEPTH, 1, D])
    bs_s = din("bs_s", [DEPTH, 1, D])
    sinks = din("sinks", [DEPTH, 16])
    sinks_s4 = din("sinks_s4", [DEPTH, 128, 4])
    bias_val = din("bias_val", [128, 16, 256])
    mask_p = din("mask_p", [128, 16, 256])
    bias_s_val = din("bias_s_val", [128, 4, 136])
    mask_s = din("mask_s", [128, 4, 136])
    fmask = din("fmask", [1, 256])
    cst = din("cst", [128, 3, 128])
    bmask_d = din("bmask", [128, 16])

    yp = dout("yp", [16 * 128, D])
    ys = dout("ys", [128, D])
    kp = dout("kp", [DEPTH, 128, 256])
    vp = dout("vp", [DEPTH, 128, 256])
    ks = dout("ks", [DEPTH, 16, 128, 256])
    vs = dout("vs", [DEPTH, 16, 128, 256])
    gv = dout("gv", [DEPTH, 128, D])

    from contextlib import ExitStack
    with ExitStack() as es:
        def sb(name, shape, dt):
            return es.enter_context(nc.sbuf_tensor(name, list(shape), dt))

        x_tok = sb("x_tok", [128, GT, D], F32)
        xT = sb("xT", [128, 8, T], BF16)
        R1 = sb("R1", [128, 3 * 8 * T], BF16)
        R2 = sb("R2", [128, NHC * D], BF16)
        bias = sb("bias", [128, 16, 256], F32)
        bias_s = sb("bias_s", [128, 4, 136], F32)
        ring = sb("ring", [128, RING, 8, BW], BF16)
        lnbuf = sb("lnbuf", [128, 2, D], F32)
        tmpv = sb("tmpv", [128, D], F32)
        gtmp = sb("gtmp", [128, 2, T], BF16)
        bstb = gtmp[0:33, :, :].rearrange("p a t -> p (a t)")[:, 0:D]
        scr2k = sb("scr2k", [128, 512], F32)
        ftmp = scr2k
        kvout = scr2k
        wsb_p = sb("wsb_p", [128, 8, 128], BF16)
        wsb_s = sb("wsb_s", [128, 8, 128], BF16)
        bsstg = sb("bsstg", [33, D], F32)
        bsrow = sb("bsrow", [33, 2, D], BF16)
        carK = sb("carK", [128, DEPTH, 2, 128], BF16)
        carV = sb("carV", [128, DEPTH, 256], BF16)
        cstf = sb("cstf", [128, 3, 128], F32)
        ident = sb("ident", [128, 128], BF16)
        ones_b = sb("ones_b", [33, 128], BF16)
        fm_b = sb("fm_b", [1, 256], BF16)
        bmask = sb("bmask_sb", [128, 16], F32)
        sink_bc = sb("sink_bc", [128, 16], F32)
        sink_s4 = sb("sink_s4", [128, 4], F32)
        nsink_bc = sb("nsink_bc", [128, 16], F32)
        st = sb("st", [128, 64], F32)
        stA = sb("stA", [128, 2, 32], F32)
        lnst = sb("lnst", [128, 64], F32)
        mv2 = sb("mv2", [128, GT, 2], F32)
        xbf = scr2k[:, :].bitcast(BF16)
        xbfs = [(xbf, "scr2k"), (tmpv[:, 0:512].bitcast(BF16), "tmpv")]

        uT = R1[:, 0:8 * T].rearrange("p (c t) -> p c t", c=8)
        va = R1[:, 8 * T:16 * T].rearrange("p (t f) -> p t f", t=GT)
        mixT = R1[:, 8 * T:16 * T].rearrange("p (c t) -> p c t", c=8)
        qT = R1[:, 16 * T:24 * T].rearrange("p (c t) -> p c t", c=8)
        hT = R1[:, 0:NHC * T].rearrange("p (c t) -> p c t", c=NHC)
        o2 = 0
        ybT = R2[:, o2:o2 + 8 * T].rearrange("p (c t) -> p c t", c=8); o2 += 8 * T
        kT = R2[:, o2:o2 + 2 * 8 * 128].rearrange("p (c t) -> p c t", c=2); o2 += 2 * 8 * 128
        Vt = R2[:, o2:o2 + 8 * 256].rearrange("p (s f) -> p s f", s=8); o2 += 8 * 256
        A0 = o2

        def r2v(off, n):
            return R2[:, A0 + off:A0 + off + n]
        s_sb = [r2v(i * 2048, 2048).bitcast(F32).rearrange("p (r k) -> p r k", r=4) for i in range(2)]
        p_sb = [r2v(4096 + i * 1024, 1024).rearrange("p (r k) -> p r k", r=4) for i in range(2)]
        pn_sb = [r2v(6144 + i * 1024, 1024).rearrange("p (r k) -> p r k", r=4) for i in range(2)]
        PT_sb = [r2v(8192 + i * 1024, 1024) for i in range(2)]
        Kc_bf = [r2v(i * 1024, 1024).rearrange("p (b f) -> p b f", b=4) for i in range(2)]
        Vc_bf = [r2v(2048 + i * 1024, 1024).rearrange("p (b f) -> p b f", b=4) for i in range(2)]
        KcT = r2v(4096, 1024).rearrange("p (b g k) -> p b g k", b=4, g=2)
        ss_sb = r2v(5120, 2 * 544).bitcast(F32).rearrange("p (b k) -> p b k", b=4)
        ps_sb = r2v(6208, 544).rearrange("p (b k) -> p b k", b=4)
        pns_sb = r2v(6752, 544).rearrange("p (b k) -> p b k", b=4)
        PTc_sb = r2v(7296, 512)
        pnew_all = r2v(7808, 128)
        PTn_sb = r2v(7936, 128)
        PTn_bd = r2v(8064, 2048).rearrange("p (b k) -> p b k", b=16)
        qs_sb = r2v(10112, 1024).rearrange("p (b g k) -> p b g k", b=16, g=2)
        wdn = R2[:, 0:NHC * D].rearrange("p (c n) -> p c n", c=NHC)

        pf = [es.enter_context(nc.psum_tensor(f"pf{i}", [128, 512], F32)) for i in range(6)]
        pbk = [es.enter_context(nc.psum_tensor(f"pb{i}", [128, 1024], BF16)) for i in range(2)]

        def sem(name):
            return es.enter_context(nc.semaphore(name))

        PE = Eng("pe", nc.tensor, sem("s_pe"))
        ACT = Eng("act", nc.scalar, sem("s_act"))
        DVE = Eng("dve", nc.vector, sem("s_dve"))
        SP = Eng("sp", nc.sync, sem("s_sp"))
        POOL = Eng("pool", nc.gpsimd, sem("s_pool"))
        NDS = 8
        spq = DmaQ(SP, [sem(f"d_sp{i}") for i in range(NDS)], "dsp")
        plq = DmaQ(POOL, [sem(f"d_pl{i}") for i in range(NDS)], "dpl")
        trk = Trk()
        OQ = plq if os.environ.get('OQ_POOL') else spq
        out_events = []

        def excl(reads, writes):
            ps = [k for k in reads if k.startswith("pf") or k.startswith("pb")]
            if ps:
                reads = [k for k in reads if k not in ps]
                writes = list(writes) + ps
            return reads, writes

        def op(eng, fn, reads=(), writes=(), extra=()):
            reads, writes = excl(reads, writes)
            d = trk.deps(reads, writes, extra)
            trk.wait(eng, d)
            ins = fn(eng.h)
            eng.cnt += 1
            ins.then_inc(eng.sem, 1)
            ev = (eng.sem, eng.cnt, eng.name)
            trk.commit(ev, reads, writes)
            return ev

        def pe(mms, reads=(), writes=(), extra=()):
            d = trk.deps(reads, writes, extra)
            trk.wait(PE, d)
            ins = None
            for m in mms:
                ins = m(nc.tensor)
            PE.cnt += 1
            ins.then_inc(PE.sem, 1)
            ev = (PE.sem, PE.cnt, "pe")
            trk.commit(ev, reads, writes)
            return ev

        def dma(q, out, in_, reads=(), writes=(), extra=(), is_out=False):
            eng = q.eng
            d = trk.deps(reads, writes, extra)
            trk.wait(eng, d)
            i = q.k % NDS
            n = q.k // NDS
            if n > 0:
                key = f"{q.tag}{i}"
                if eng.waited.get(key, 0) < 16 * n:
                    eng.h.wait_ge(q.sems[i], 16 * n)
                    eng.waited[key] = 16 * n
            ins = eng.h.dma_start(out=out, in_=in_)
            ins.then_inc(q.sems[i], 16)
            ev = (q.sems[i], 16 * (n + 1), f"{q.tag}{i}")
            q.k += 1
            trk.commit(ev, reads, writes)
            if is_out:
                out_events.append(ev)
            return ev

        def last_ev(eng):
            return (eng.sem, eng.cnt, eng.name) if eng.cnt > 0 else None

        pfi = [0]
        pbi = [0]

        resv = set()

        def nf():
            while True:
                i = pfi[0] % 6
                pfi[0] += 1
                if i not in resv:
                    return pf[i], f"pf{i}"

        def nb():
            i = pbi[0] % 2
            pbi[0] += 1
            return pbk[i], f"pb{i}"

        alt = [0]

        def evac_eng():
            alt[0] ^= 1
            return ACT if alt[0] else DVE

        def copy(eng, out, in_, reads, writes, scale=None):
            if eng is ACT:
                if scale is None:
                    return op(ACT, lambda h: h.copy(out=out, in_=in_), reads, writes)
                return op(ACT, lambda h: h.activation(out=out, in_=in_, func=AF.Copy, scale=scale), reads, writes)
            if scale is None:
                return op(DVE, lambda h: h.tensor_copy(out=out, in_=in_), reads, writes)
            return op(DVE, lambda h: h.tensor_scalar(out=out, in0=in_, scalar1=scale, scalar2=None, op0=ALU.mult), reads, writes)

        class WStream:
            def __init__(self):
                self.blocks = []
                self.issued = 0

            def add(self, ap, nk, ncols):
                self.blocks.append((ap, nk, ncols))
                return len(self.blocks) - 1

            def issue_upto(self, n):
                while self.issued < min(n, len(self.blocks)):
                    i = self.issued
                    ap, nk, ncols = self.blocks[i]
                    s = i % RING
                    dma(plq, ring[:, s, 0:nk, 0:ncols], ap, reads=(), writes=(f"ring{s}",))
                    self.issued += 1

            def slot(self, i):
                self.issue_upto(i + 1)
                return i % RING

            def done(self, i):
                self.issue_upto(i + RING + 1)

        ws = WStream()

        def wview(w, l, c0, ncols, k0=0, nk=8):
            return w[l].rearrange("(kc p) n -> p kc n", p=128)[:, k0:k0 + nk, c0:c0 + ncols]

        plan = {}
        for G in groups:
            for l in range(depth_run):
                b = {}
                b["q"] = [ws.add(wview(w_in, l, O_Q + i * BW, BW), 8, BW) for i in range(4)]
                b["kv"] = [ws.add(wview(w_in, l, O_K + i * BW, BW), 8, BW) for i in range(2)]
                b["u"] = [ws.add(wview(w_in, l, O_U + i * BW, BW), 8, BW) for i in range(2)]
                b["v"] = [ws.add(wview(w_in, l, O_V + i * BW, BW), 8, BW) for i in range(4)]
                b["u"] += [ws.add(wview(w_in, l, O_U + i * BW, BW), 8, BW) for i in range(2, 4)]
                b["ga_pa"] = []
                for i in range(4):
                    b["ga_pa"].append((ws.add(wview(w_in, l, O_G + i * BW, BW), 8, BW),
                                       ws.add(wview(w_pa, l, i * BW, BW), 8, BW)))
                b["gb_pb"] = []
                for i in range(4):
                    b["gb_pb"].append((ws.add(wview(w_in, l, O_G + D + i * BW, BW), 8, BW),
                                       ws.add(wview(w_pb, l, i * BW, BW), 8, BW)))
                b["wo"] = [ws.add(wview(w_o, l, i * BW, BW), 8, BW) for i in range(4)]
                b["gu"] = []
                for i in range(11):
                    b["gu"].append((ws.add(wview(w_gate, l, i * BW, BW), 8, BW),
                                    ws.add(wview(w_up, l, i * BW, BW), 8, BW)))
                plan[(G, l)] = b

        ws.issue_upto(RING)
        def K(name, *idx):
            return name + ":" + ":".join(str(i) for i in idx)

        dma(spq, cstf[:], cst, writes=("c:cstf",))
        copy(DVE, ident[:], cstf[:, 0, :], ("c:cstf",), ("c:ident",))
        op(DVE, lambda h: h.memset(ones_b[:], 1.0), (), ("c:ones",))
        op(DVE, lambda h: h.memset(bsrow[:], 0.0), (), ("bsrow",))
        op(DVE, lambda h: h.memset(R2[:, 8 * T:A0], 0.0), (), ("kTV",))
        op(DVE, lambda h: h.memset(carK[:], 0.0), (), ("carK",))
        op(DVE, lambda h: h.memset(carV[:], 0.0), (), ("carV",))
        op(DVE, lambda h: h.memset(mv2[:], 1.0), (), ("lnst",))
        xk = lambda tl: K("x", tl)
        for tl in range(GT):
            t_ = groups[0] * GT + tl
            dma(spq, x_tok[:, tl, :], xin[t_ * 128:(t_ + 1) * 128, :], writes=(xk(tl),))
        dma(spq, bsstg[0:1, 0:256], fmask, writes=("bsstg",))
        copy(DVE, fm_b[:], bsstg[0:1, 0:256], ("bsstg",), ("c:fm_b",))
        dma(spq, bmask[:], bmask_d, writes=("c:bmask",))
        dma(spq, bias[:], bias_val, writes=("bias",))
        for hh in range(0, 16, 4):
            dma(spq, lnbuf[:, 0, :].rearrange("p (h k) -> p h k", h=4), mask_p[:, hh:hh + 4, :], writes=("lnbuf",))
            op(DVE, lambda h, hh=hh: h.tensor_tensor(out=bias[:, hh:hh + 4, :], in0=bias[:, hh:hh + 4, :],
                                                      in1=lnbuf[:, 0, :].rearrange("p (h k) -> p h k", h=4), op=ALU.add),
               ("bias", "lnbuf"), ("bias",))
        dma(spq, bias_s[:], bias_s_val, writes=("bias_s",))
        dma(spq, lnbuf[:, 1, 0:544].rearrange("p (b k) -> p b k", b=4), mask_s, writes=("lnbuf",))
        op(DVE, lambda h: h.tensor_tensor(out=bias_s[:], in0=bias_s[:], in1=lnbuf[:, 1, 0:544].rearrange("p (b k) -> p b k", b=4), op=ALU.add),
           ("bias_s", "lnbuf"), ("bias_s",))
        trk.lw["c:bias"] = trk.lw["bias"]
        trk.lw["c:bias_s"] = trk.lw["bias_s"]

        def rsqrt_col(var_ap, eps, out_ap):
            kk = ("st",)
            op(DVE, lambda h: h.tensor_scalar(out=st[:, 40:41], in0=var_ap, scalar1=eps, scalar2=None, op0=ALU.add), kk, kk)
            op(DVE, lambda h: h.tensor_copy(out=st[:, 41:42], in_=st[:, 40:41].bitcast(I32)), kk, kk)
            op(DVE, lambda h: h.tensor_scalar(out=st[:, 42:43], in0=st[:, 41:42], scalar1=-0.5, scalar2=float(0x5f3759df), op0=ALU.mult, op1=ALU.add), kk, kk)
            op(DVE, lambda h: h.tensor_copy(out=out_ap.bitcast(I32), in_=st[:, 42:43]), kk, kk)
            op(DVE, lambda h: h.tensor_scalar(out=st[:, 44:45], in0=st[:, 40:41], scalar1=-0.5, scalar2=None, op0=ALU.mult), kk, kk)
            for _ in range(3):
                op(DVE, lambda h: h.scalar_tensor_tensor(out=st[:, 43:44], in0=out_ap, scalar=out_ap, in1=st[:, 44:45], op0=ALU.mult, op1=ALU.mult), kk, kk)
                op(DVE, lambda h: h.scalar_tensor_tensor(out=out_ap, in0=st[:, 43:44], scalar=1.5, in1=out_ap, op0=ALU.add, op1=ALU.mult), kk, kk)

        def layer_norm(src, skey, eps, out2=None, out2_keys=()):
            kk = ("st",)
            op(DVE, lambda h: h.bn_stats(out=st[:, 0:6], in_=src[:, 0:512]), (skey,) + kk, kk)
            op(DVE, lambda h: h.bn_stats(out=st[:, 6:12], in_=src[:, 512:1024]), (skey,) + kk, kk)
            op(DVE, lambda h: h.bn_aggr(out=st[:, 12:14], in_=st[:, 0:12]), kk, kk)
            rsqrt_col(st[:, 13:14], eps, st[:, 14:15])
            op(DVE, lambda h: h.scalar_tensor_tensor(out=src, in0=src, scalar=st[:, 12:13], in1=lnbuf[:, 0, :],
                                                      op0=ALU.subtract, op1=ALU.mult), (skey, "lnbuf") + kk, (skey,))
            dst = src if out2 is None else out2
            op(DVE, lambda h: h.scalar_tensor_tensor(out=dst, in0=src, scalar=st[:, 14:15], in1=lnbuf[:, 1, :],
                                                      op0=ALU.mult, op1=ALU.add), (skey, "lnbuf") + kk, (skey,) if out2 is None else tuple(out2_keys))

        def ln_stats(tl):
            kk = ("lnst",)
            src = x_tok[:, tl, :]
            op(DVE, lambda h: h.bn_stats(out=lnst[:, 0:6], in_=src[:, 0:512]), (xk(tl),) + kk, kk)
            op(DVE, lambda h: h.bn_stats(out=lnst[:, 6:12], in_=src[:, 512:1024]), (xk(tl),) + kk, kk)
            op(DVE, lambda h: h.bn_aggr(out=mv2[:, tl, :], in_=lnst[:, 0:12]), kk, kk)

        def ln_group(tls, eps, want_xT=True):
            kk = ("lnst",)
            var = mv2[:, :, 1]
            vv, ti, tf, yy, tt = lnst[:, 16:23], lnst[:, 24:31], lnst[:, 32:39], lnst[:, 40:47], lnst[:, 48:55]
            op(DVE, lambda h: h.tensor_scalar(out=vv, in0=var, scalar1=eps, scalar2=None, op0=ALU.add), kk, kk)
            op(DVE, lambda h: h.tensor_copy(out=ti, in_=vv.bitcast(I32)), kk, kk)
            op(DVE, lambda h: h.tensor_scalar(out=tf, in0=ti, scalar1=-0.5, scalar2=float(0x5f3759df), op0=ALU.mult, op1=ALU.add), kk, kk)
            op(DVE, lambda h: h.tensor_copy(out=yy.bitcast(I32), in_=tf), kk, kk)
            op(DVE, lambda h: h.tensor_scalar(out=vv, in0=vv, scalar1=-0.5, scalar2=None, op0=ALU.mult), kk, kk)
            for _ in range(3):
                op(DVE, lambda h: h.tensor_tensor(out=tt, in0=yy, in1=yy, op=ALU.mult), kk, kk)
                op(DVE, lambda h: h.tensor_tensor(out=tt, in0=tt, in1=vv, op=ALU.mult), kk, kk)
                op(DVE, lambda h: h.scalar_tensor_tensor(out=yy, in0=tt, scalar=1.5, in1=yy, op0=ALU.add, op1=ALU.mult), kk, kk)
            for tl in tls:
                src = x_tok[:, tl, :]
                op(DVE, lambda h, src=src, tl=tl: h.scalar_tensor_tensor(out=src, in0=src, scalar=mv2[:, tl, 0:1], in1=lnbuf[:, 0, :],
                                                                          op0=ALU.subtract, op1=ALU.mult), (xk(tl), "lnbuf") + kk, (xk(tl),))
                if want_xT:
                    i = xbi[0] % 2
                    xbi[0] += 1
                    xb, xkey = xbfs[i]
                    op(DVE, lambda h, src=src, tl=tl, xb=xb: h.scalar_tensor_tensor(out=xb, in0=src, scalar=lnst[:, 40 + tl:41 + tl], in1=lnbuf[:, 1, :],
                                                                                 op0=ALU.mult, op1=ALU.add), (xk(tl), "lnbuf") + kk, (xkey,))
                    bank, bkey = nb()
                    pe([lambda t, c=c, xb=xb: t.transpose(out=bank[:, c * 128:(c + 1) * 128], in_=xb[:, c * 128:(c + 1) * 128], identity=ident[:])
                        for c in range(8)], (xkey, "c:ident"), (bkey,))
                    copy(ACT, xT[:, :, tl * 128:(tl + 1) * 128], bank[:, :].rearrange("p (c t) -> p c t", c=8), (bkey,), [K("xT", tl)])

                def passB(src=src, tl=tl):
                    op(DVE, lambda h: h.scalar_tensor_tensor(out=src, in0=src, scalar=lnst[:, 40 + tl:41 + tl], in1=lnbuf[:, 1, :],
                                                              op0=ALU.mult, op1=ALU.add), (xk(tl), "lnbuf") + kk, (xk(tl),))
                if want_xT:
                    deferred.append(passB)
                else:
                    passB()

        deferred = []
        xbi = [0]

        def run_deferred(n=1):
            for _ in range(n):
                if deferred:
                    deferred.pop(0)()

        def flush_deferred():
            run_deferred(len(deferred))

        def load_ln(l, which):
            flush_deferred()
            dma(spq, lnbuf[:, 0, :], lnp[l, 2 * which:2 * which + 1, :].partition_broadcast(128), writes=("lnbuf",))
            dma(spq, lnbuf[:, 1, :], lnp[l, 2 * which + 1:2 * which + 2, :].partition_broadcast(128), writes=("lnbuf",))

        def make_xT(tl):
            copy(ACT, xbf, x_tok[:, tl, :], (xk(tl),), ("scr2k",))
            bank, bkey = nb()
            pe([lambda t, c=c: t.transpose(out=bank[:, c * 128:(c + 1) * 128], in_=xbf[:, c * 128:(c + 1) * 128], identity=ident[:])
                for c in range(8)], ("scr2k", "c:ident"), (bkey,))
            copy(evac_eng(), xT[:, :, tl * 128:(tl + 1) * 128], bank[:, :].rearrange("p (c t) -> p c t", c=8),
                 (bkey,), [K("xT", tl)])

        def segs_of(active):
            return [active[i:i + 4] for i in range(0, len(active), 4)]

        def cols(seg):
            return slice(seg[0] * 128, (seg[-1] + 1) * 128)

        def ws_job(slot_key, lhs_fn, rhs, rkeys, ncol):
            bank, bkey = nf()
            pe([lambda t, kc=kc: t.matmul(bank[:, 0:ncol], lhsT=lhs_fn(kc), rhs=rhs(kc), start=(kc == 0), stop=(kc == 7))
                for kc in range(8)], [slot_key] + list(rkeys), (bkey,))
            wsj[0] += 1
            if wsj[0] % 3 == 0:
                run_deferred(1)
            return bank, bkey

        wsj = [0]

        c_res = 1.0 / ALPHA
        eps_res = LN_EPS / (ALPHA * ALPHA)

        for G in groups:
            tiles_g = [G * GT + i for i in range(GT)]
            for l in range(depth_run):
                pl = plan[(G, l)]
                active = list(range(l, GT)) if G == 0 else list(range(GT))
                segs = segs_of(active)
                has_sample = (G == 2) and not os.environ.get('NO_SAMPLE')
                gt = lambda tl: G * GT + tl

                dma(spq, sink_bc[:], sinks[l:l + 1, :].partition_broadcast(128), writes=("sink_bc",))
                op(DVE, lambda h: h.tensor_scalar(out=nsink_bc[:], in0=sink_bc[:], scalar1=-1.0, scalar2=None, op0=ALU.mult), ("sink_bc",), ("sink_bc",))
                dma(spq, tmpv[:].rearrange("p (g t) -> p g t", g=8), wsT_p[l], writes=("tmpv",))
                for gi in range(8):
                    op(DVE, lambda h, gi=gi: h.tensor_tensor(out=wsb_p[:, gi, :], in0=tmpv[:, gi * 128:(gi + 1) * 128], in1=cstf[:, 1, :], op=ALU.mult),
                       ("tmpv", "c:cstf"), ("wsb_p",))
                def bias_rows(src, bi_):
                    dma(spq, bsstg[0:1, :], src, writes=("bsstg",))
                    dma(spq, bsstg[32:33, :], src, writes=("bsstg",))
                    op(DVE, lambda h: h.tensor_copy(out=bsrow[0:1, bi_, :], in_=bsstg[0:1, :]), ("bsstg",), ("bsrow",))
                    op(DVE, lambda h: h.tensor_copy(out=bstb[32:33, :], in_=bsstg[32:33, :]), ("bsstg",), ("gtmp0", "gtmp1"))
                    op(DVE, lambda h: h.tensor_tensor(out=bsstg[32:33, :], in0=bsstg[32:33, :], in1=bstb[32:33, :], op=ALU.subtract),
                       ("bsstg", "gtmp0", "gtmp1"), ("bsstg",))
                    op(DVE, lambda h: h.tensor_copy(out=bsrow[32:33, bi_, :], in_=bsstg[32:33, :]), ("bsstg",), ("bsrow",))
                bias_rows(bs_p[l], 0)
                if has_sample:
                    dma(spq, sink_s4[:], sinks_s4[l], writes=("sink_s4",))
                    dma(spq, tmpv[:].rearrange("p (g t) -> p g t", g=8), wsT_s[l], writes=("tmpv",))
                    for gi in range(8):
                        op(DVE, lambda h, gi=gi: h.tensor_tensor(out=wsb_s[:, gi, :], in0=tmpv[:, gi * 128:(gi + 1) * 128], in1=cstf[:, 2, :], op=ALU.mult),
                           ("tmpv", "c:cstf"), ("wsb_s",))
                    bias_rows(bs_s[l], 1)

                if l == 0:
                    if G != groups[0]:
                        for tl in active:
                            dma(spq, x_tok[:, tl, :], xin[gt(tl) * 128:(gt(tl) + 1) * 128, :], writes=(xk(tl),))
                    for tl in active:
                        make_xT(tl)
                    if os.environ.get('TEST_GV0'):
                        if os.environ.get('TEST_GV0') == '1':
                            dma(spq, gv[1], x_tok[:, active[0], :], reads=(xk(active[0]),), is_out=True)
                        elif os.environ.get('TEST_GV0') == '2':
                            for q_ in range(4):
                                dma(spq, gv[1, :, q_ * 256:(q_ + 1) * 256], x_tok[:, active[0], q_ * 256:(q_ + 1) * 256], reads=(xk(active[0]),), is_out=True)
                        elif os.environ.get('TEST_GV0') == '3':
                            dma(spq, yp[0:128, :], x_tok[:, active[0], :], reads=(xk(active[0]),), is_out=True)
                        elif os.environ.get('TEST_GV0') == '4':
                            dma(spq, gv[1, :, 0:512], x_tok[:, active[0], 0:512], reads=(xk(active[0]),), is_out=True)
                    if os.environ.get('TEST_KP0'):
                        dma(spq, kp[1], x_tok[:, active[0], 0:256], reads=(xk(active[0]),), is_out=True)

                xkeys = lambda seg: [K("xT", tl) for tl in seg]
                active_kv, segs_kv = active, segs
                if G == 0:
                    active = active[1:]
                    segs = segs_of(active)

                for bi, blk in enumerate(pl["q"]):
                    s = ws.slot(blk)
                    for cc in range(2):
                        c = bi * 2 + cc
                        for seg in segs:
                            n = len(seg) * 128
                            bank, bkey = ws_job(f"ring{s}", lambda kc, s=s, cc=cc: ring[:, s, kc, cc * 128:(cc + 1) * 128],
                                                lambda kc, seg=seg: xT[:, kc, cols(seg)], xkeys(seg), n)
                            copy(ACT, qT[:, c, cols(seg)], bank[:, 0:n], (bkey,), [K("qT", c, tl) for tl in seg], scale=0.125)
                    ws.done(blk)
                copy(DVE, kT[:, :, 0:128], carK[:, l, :, :], ("carK",), [K("kT", 0)])
                copy(DVE, Vt[:, 0, :], carV[:, l, :], ("carV",), [K("V", 0)])
                s = ws.slot(pl["kv"][0])
                s_v = ws.slot(pl["kv"][1])
                for c in range(2):
                    for seg in segs_kv:
                        n = len(seg) * 128
                        bank, bkey = ws_job(f"ring{s}", lambda kc, s=s, c=c: ring[:, s, kc, c * 128:(c + 1) * 128],
                                            lambda kc, seg=seg: xT[:, kc, cols(seg)], xkeys(seg), n)
                        copy(evac_eng(), kT[:, c, (seg[0] + 1) * 128:(seg[-1] + 2) * 128], bank[:, 0:n], (bkey,), [K("kT", tl + 1) for tl in seg])
                for tl in active_kv:
                    bank, bkey = nf()
                    pe([lambda t, kc=kc, tl=tl, q2=q2: t.matmul(bank[:, q2 * BW:(q2 + 1) * BW], lhsT=xT[:, kc, tl * 128:(tl + 1) * 128],
                                                                 rhs=ring[:, (s, s_v)[q2], kc, :], start=(kc == 0), stop=(kc == 7))
                        for q2 in range(2) for kc in range(8)],
                       [f"ring{s}", f"ring{s_v}", K("xT", tl)], (bkey,))
                    copy(DVE, Vt[:, tl + 1, :], bank[:, 256:512], (bkey,), [K("V", tl + 1)])
                    if gt(tl) in (LASTP_T, SAMPLE_T) and not os.environ.get('NO_KV'):
                        copy(DVE, kvout[:], bank[:, :], (bkey,), ("scr2k",))
                        if gt(tl) == LASTP_T:
                            dma(OQ, kp[l], kvout[:, 0:256], reads=("scr2k",), is_out=True)
                            dma(OQ, vp[l], kvout[:, 256:512], reads=("scr2k",), is_out=True)
                        elif not os.environ.get('SKIP_KSNEW'):
                            for b_ in range(16):
                                dma(spq, ks[l, b_, 120:128, :], kvout[b_ * 8:(b_ + 1) * 8, 0:256], reads=("scr2k",), is_out=True)
                                dma(spq, vs[l, b_, 120:128, :], kvout[b_ * 8:(b_ + 1) * 8, 256:512], reads=("scr2k",), is_out=True)
                ws.done(pl["kv"][0])
                ws.done(pl["kv"][1])

                def stageA(tl, g, bi_):
                    sl = tl + 1
                    tcols = slice(tl * 128, (tl + 1) * 128)
                    gp, half = g // 2, g % 2
                    rows = slice(half * 64, half * 64 + 64)
                    sbuf_s, sbuf_p, sbuf_pn = s_sb[bi_], p_sb[bi_], pn_sb[bi_]
                    sk, pk, pnk = f"s_sb{bi_}", f"p_sb{bi_}", f"pn_sb{bi_}"
                    sa = stA[:, bi_, :]
                    kk = (f"stA{bi_}",)
                    first = (gt(tl) == FIRST_T)
                    banks = []
                    for rp in range(2):
                        bank, bkey = nf()
                        mms = []
                        for rr in range(2):
                            r = rp * 2 + rr
                            mms.append(lambda t, r=r, rr=rr, bank=bank: t.matmul(bank[:, rr * 256:(rr + 1) * 256], lhsT=qT[rows, gp * 4 + r, tcols],
                                                                                  rhs=kT[rows, gp, (sl - 1) * 128:(sl + 1) * 128], start=True, stop=not first))
                            if first:
                                mms.append(lambda t, rr=rr, bank=bank: t.matmul(bank[:, rr * 256:(rr + 1) * 256], lhsT=ones_b[0:1, :], rhs=fm_b[0:1, :],
                                                                                start=False, stop=True))
                        pe(mms, [K("qT", gp * 4 + rp * 2, tl), K("qT", gp * 4 + rp * 2 + 1, tl), K("kT", sl - 1), K("kT", sl), "c:ones", "c:fm_b"], (bkey,))
                        banks.append((bank, bkey))
                    for rp, (bank, bkey) in enumerate(banks):
                        op(DVE, lambda h, bank=bank, rp=rp: h.tensor_tensor(out=sbuf_s[:, rp * 2:rp * 2 + 2, :], in0=bank[:, :].rearrange("p (r k) -> p r k", r=2),
                                                                            in1=bias[:, 4 * g + rp * 2:4 * g + rp * 2 + 2, :], op=ALU.add),
                           (bkey, "c:bias"), (sk,))
                    op(DVE, lambda h: h.tensor_reduce(out=sa[:, 0:4], in_=sbuf_s[:, :, :], axis=AX.X, op=ALU.max), (sk,) + kk, kk)
                    op(DVE, lambda h: h.scalar_tensor_tensor(out=sa[:, 4:8], in0=sa[:, 0:4], scalar=-1.0, in1=nsink_bc[:, 4 * g:4 * g + 4],
                                                             op0=ALU.mult, op1=ALU.min), ("sink_bc",) + kk, kk)
                    op(DVE, lambda h: h.tensor_tensor(out=sa[:, 8:12], in0=sink_bc[:, 4 * g:4 * g + 4], in1=sa[:, 4:8], op=ALU.add), ("sink_bc",) + kk, kk)
                    return (tl, g, bi_)

                def stageA2e(ctx):
                    tl, g, bi_ = ctx
                    sbuf_s, sbuf_p, sbuf_pn = s_sb[bi_], p_sb[bi_], pn_sb[bi_]
                    sk, pk, pnk = f"s_sb{bi_}", f"p_sb{bi_}", f"pn_sb{bi_}"
                    sa = stA[:, bi_, :]
                    kk = (f"stA{bi_}",)
                    for r in range(4):
                        op(ACT, lambda h, r=r: h.activation(out=sbuf_p[:, r, :], in_=sbuf_s[:, r, :], func=AF.Exp, bias=sa[:, 4 + r:5 + r],
                                                             accum_out=sa[:, 12 + r:13 + r]), (sk,) + kk, (pk,) + kk)
                    op(ACT, lambda h: h.activation(out=sa[:, 8:12], in_=sa[:, 8:12], func=AF.Exp), kk, kk)

                def stageA2d(ctx):
                    tl, g, bi_ = ctx
                    sbuf_s, sbuf_p, sbuf_pn = s_sb[bi_], p_sb[bi_], pn_sb[bi_]
                    sk, pk, pnk = f"s_sb{bi_}", f"p_sb{bi_}", f"pn_sb{bi_}"
                    sa = stA[:, bi_, :]
                    kk = (f"stA{bi_}",)
                    op(DVE, lambda h: h.tensor_tensor(out=sa[:, 16:20], in0=sa[:, 12:16], in1=sa[:, 8:12], op=ALU.add), kk, kk)
                    op(DVE, lambda h: h.reciprocal(out=sa[:, 20:24], in_=sa[:, 16:20]), kk, kk)
                    op(DVE, lambda h: h.tensor_tensor(out=sbuf_pn[:, :, :], in0=sbuf_p[:, :, :],
                                                      in1=sa[:, 20:24].unsqueeze(2).broadcast_to([128, 4, 256]), op=ALU.mult),
                       (pk,) + kk, (pnk,))

                OT = {}

                def stageB1(ctx):
                    tl, g, bi_ = ctx
                    sbuf_pn, sbuf_pt = pn_sb[bi_], PT_sb[bi_]
                    pnk, ptk = f"pn_sb{bi_}", f"PT_sb{bi_}"
                    bankT, btkey = nb()
                    pe([lambda t, kb=kb, r=r: t.transpose(out=bankT[:, (kb * 4 + r) * 128:(kb * 4 + r + 1) * 128],
                                                            in_=sbuf_pn[:, r, kb * 128:(kb + 1) * 128], identity=ident[:])
                        for kb in range(2) for r in range(4)], (pnk, "c:ident"), (btkey,))
                    copy(ACT, sbuf_pt[:, :], bankT[:, :], (btkey,), (ptk,))

                def stageB2(ctx):
                    tl, g, bi_ = ctx
                    sl = tl + 1
                    tcols = slice(tl * 128, (tl + 1) * 128)
                    gp, half = g // 2, g % 2
                    rows = slice(half * 64, half * 64 + 64)
                    sbuf_pt = PT_sb[bi_]
                    ptk = f"PT_sb{bi_}"
                    if half == 0:
                        OT[(tl, gp)] = nf()
                        resv.add(int(OT[(tl, gp)][1][2:]))
                    obank, okey = OT[(tl, gp)]
                    pe([lambda t, kb=kb: t.matmul(obank[rows, :], lhsT=Vt[:, sl - 1 + kb, g * 64:(g + 1) * 64], rhs=sbuf_pt[:, kb * 512:(kb + 1) * 512],
                                                   start=(kb == 0), stop=(kb == 1), tile_position=(0, half * 64)) for kb in range(2)],
                       (ptk, K("V", sl - 1), K("V", sl)), (okey,))

                def stageB3(ctx):
                    tl, g, bi_ = ctx
                    tcols = slice(tl * 128, (tl + 1) * 128)
                    gp, half = g // 2, g % 2
                    if half == 1:
                        obank, okey = OT[(tl, gp)]
                        copy(ACT, ybT[:, gp * 4:gp * 4 + 4, tcols], obank[:, :].rearrange("p (r q) -> p r q", r=4), (okey,),
                             [K("ybT", gp * 4 + r, tl) for r in range(4)])
                        resv.discard(int(okey[2:]))

                def attn_gen():
                    units = [(tl, g) for tl in active if gt(tl) != SAMPLE_T for g in range(4)]
                    p1 = p2 = p3 = None
                    for i, u_ in enumerate(units + [None, None, None]):
                        ctx = None
                        if u_ is not None:
                            tl, g = u_
                            ctx = stageA(tl, g, i % 2)
                        if p1 is not None:
                            stageA2d(p1)
                        if p2 is not None:
                            stageB1(p2)
                        yield
                        if ctx is not None:
                            stageA2e(ctx)
                        if p3 is not None:
                            stageB2(p3)
                            stageB3(p3)
                        p3, p2, p1 = p2, p1, ctx
                        yield

                def dense_gen(phase):
                    key_ = ("ga_pa", "gb_pb")[phase]
                    src = uT if phase == 0 else ybT
                    sname = "uT" if phase == 0 else "ybT"
                    for bi, (gblk, pblk) in enumerate(pl[key_]):
                        sg, spj = ws.slot(gblk), ws.slot(pblk)
                        for cc in range(2):
                            c = bi * 2 + cc
                            for seg in segs:
                                n = len(seg) * 128
                                gb_ = (c * 2 + segs.index(seg)) % 2
                                bank, bkey = ws_job(f"ring{sg}", lambda kc, sg=sg, cc=cc: ring[:, sg, kc, cc * 128:(cc + 1) * 128],
                                                    lambda kc, seg=seg: xT[:, kc, cols(seg)], xkeys(seg), n)
                                op(ACT, lambda h, bank=bank, gb_=gb_, seg=seg, n=n: h.activation(out=gtmp[:, gb_, cols(seg)], in_=bank[:, 0:n], func=AF.Sigmoid),
                                   (bkey,), (f"gtmp{gb_}",))
                                bank2, bkey2 = ws_job(f"ring{spj}", lambda kc, spj=spj, cc=cc: ring[:, spj, kc, cc * 128:(cc + 1) * 128],
                                                      lambda kc, seg=seg: src[:, kc, cols(seg)],
                                                      [K(sname, kc, tl) for kc in range(8) for tl in seg], n)
                                mkeys = [K("mixT", c, tl) for tl in seg]
                                if phase == 0:
                                    op(DVE, lambda h, bank2=bank2, gb_=gb_, seg=seg, n=n, c=c: h.tensor_tensor(out=mixT[:, c, cols(seg)], in0=bank2[:, 0:n],
                                                                                                          in1=gtmp[:, gb_, cols(seg)], op=ALU.mult),
                                       (bkey2, f"gtmp{gb_}"), mkeys)
                                else:
                                    op(DVE, lambda h, bank2=bank2, gb_=gb_, seg=seg, n=n: h.tensor_tensor(out=ftmp[:, 0:n], in0=bank2[:, 0:n],
                                                                                                     in1=gtmp[:, gb_, cols(seg)], op=ALU.mult),
                                       (bkey2, f"gtmp{gb_}"), ("scr2k",))
                                    op(DVE, lambda h, seg=seg, n=n, c=c: h.tensor_tensor(out=mixT[:, c, cols(seg)], in0=ftmp[:, 0:n],
                                                                                       in1=mixT[:, c, cols(seg)], op=ALU.add),
                                       ["scr2k"] + mkeys, mkeys)
                                yield
                        ws.done(gblk)
                        ws.done(pblk)

                def pre_gen():
                    for bi, blk in list(enumerate(pl["u"]))[:2]:
                        s = ws.slot(blk)
                        for cc in range(2):
                            c = bi * 2 + cc
                            for seg in segs:
                                n = len(seg) * 128
                                bank, bkey = ws_job(f"ring{s}", lambda kc, s=s, cc=cc: ring[:, s, kc, cc * 128:(cc + 1) * 128],
                                                    lambda kc, seg=seg: xT[:, kc, cols(seg)], xkeys(seg), n)
                                op(ACT, lambda h, bank=bank, c=c, seg=seg, n=n: h.activation(out=uT[:, c, cols(seg)], in_=bank[:, 0:n], func=AF.Gelu_apprx_tanh),
                                   (bkey,), [K("uT", c, tl) for tl in seg])
                                yield
                        ws.done(blk)

                    def u_steps():
                        for bi, blk in list(enumerate(pl["u"]))[2:]:
                            s = ws.slot(blk)
                            for cc in range(2):
                                c = bi * 2 + cc
                                for seg in segs:
                                    n = len(seg) * 128
                                    bank, bkey = ws_job(f"ring{s}", lambda kc, s=s, cc=cc: ring[:, s, kc, cc * 128:(cc + 1) * 128],
                                                        lambda kc, seg=seg: xT[:, kc, cols(seg)], xkeys(seg), n)
                                    op(ACT, lambda h, bank=bank, c=c, seg=seg, n=n: h.activation(out=uT[:, c, cols(seg)], in_=bank[:, 0:n], func=AF.Gelu_apprx_tanh),
                                       (bkey,), [K("uT", c, tl) for tl in seg])
                                    yield


                    load_ln(l, 0)
                    vs_ = [ws.slot(b_) for b_ in pl["v"]]
                    ug_ = u_steps()
                    u_alive = True
                    for tl in active:
                        for hf in range(2):
                            bank, bkey = nf()
                            pe([lambda t, kc=kc, q2=q2, tl=tl: t.matmul(bank[:, q2 * BW:(q2 + 1) * BW], lhsT=xT[:, kc, tl * 128:(tl + 1) * 128],
                                                                          rhs=ring[:, vs_[hf * 2 + q2], kc, :], start=(kc == 0), stop=(kc == 7))
                                for q2 in range(2) for kc in range(8)],
                               [f"ring{vs_[hf * 2]}", f"ring{vs_[hf * 2 + 1]}", K("xT", tl)], (bkey,))
                            op(ACT, lambda h, bank=bank, hf=hf: h.activation(out=tmpv[:, hf * 512:(hf + 1) * 512], in_=bank[:, :], func=AF.Gelu_apprx_tanh),
                               (bkey,), ("tmpv",))
                        if gt(tl) == SAMPLE_T:
                            layer_norm(tmpv[:], "tmpv", LN_EPS)
                            copy(DVE, va[:, tl, :], tmpv[:], ("tmpv",), [K("va", tl)])
                        else:
                            layer_norm(tmpv[:], "tmpv", LN_EPS, out2=va[:, tl, :], out2_keys=[K("va", tl)])
                        if gt(tl) == SAMPLE_T and not os.environ.get('NO_GV'):
                            [dma(OQ, gv[l, :, q_ * 256:(q_ + 1) * 256], tmpv[:, q_ * 256:(q_ + 1) * 256], reads=("tmpv",), is_out=True) for q_ in range(4)]
                        yield
                        if u_alive:
                            try:
                                next(ug_)
                                yield
                            except StopIteration:
                                u_alive = False
                    for _ in ug_:
                        yield
                    for b_ in pl["v"]:
                        ws.done(b_)
                    for b_ in pl["u"][2:]:
                        ws.done(b_)

                    for tl in active:
                        smp = gt(tl) == SAMPLE_T
                        wsb = wsb_s if smp else wsb_p
                        wk = "wsb_s" if smp else "wsb_p"
                        bi_ = 1 if smp else 0
                        for hf in range(2):
                            bank, bkey = nf()
                            mms = []
                            for gg in range(4):
                                gi = hf * 4 + gg
                                mms.append(lambda t, gi=gi, gg=gg, tl=tl: t.matmul(bank[:, gg * 128:(gg + 1) * 128], lhsT=va[:, tl, gi * 128:(gi + 1) * 128],
                                                                                  rhs=wsb[:, gi, :], start=True, stop=False))
                                mms.append(lambda t, gi=gi, gg=gg: t.matmul(bank[:, gg * 128:(gg + 1) * 128], lhsT=ones_b[0:33, :],
                                                                           rhs=bsrow[0:33, bi_, gi * 128:(gi + 1) * 128], start=False, stop=True))
                            pe(mms, [K("va", tl), wk, "bsrow", "c:ones"], (bkey,))
                            ukeys = [K("uT", hf * 4 + gg, tl) for gg in range(4)]
                            op(DVE, lambda h, bank=bank, hf=hf, tl=tl: h.tensor_tensor(out=uT[:, hf * 4:hf * 4 + 4, tl * 128:(tl + 1) * 128],
                                                                                       in0=bank[:, :].rearrange("p (g t) -> p g t", g=4),
                                                                                       in1=uT[:, hf * 4:hf * 4 + 4, tl * 128:(tl + 1) * 128], op=ALU.mult),
                               [bkey] + ukeys, ukeys)
                            yield

                    yield from dense_gen(0)

                ag_, pg_ = attn_gen(), pre_gen()
                a_alive = p_alive = True
                while a_alive or p_alive:
                    if a_alive:
                        try:
                            next(ag_)
                        except StopIteration:
                            a_alive = False
                    for _ in range(1):
                        if p_alive:
                            try:
                                next(pg_)
                            except StopIteration:
                                p_alive = False

                if has_sample and not os.environ.get('SKIP_SATT'):
                    evs_ = [e for e in (last_ev(PE), last_ev(ACT), last_ev(DVE)) if e]
                    for eng_ in (PE, ACT, DVE):
                        trk.wait(eng_, trk.deps((), (), evs_))
                    tl = GT - 1
                    sl = tl + 1
                    scol0 = tl * 128
                    OTs = [nf(), nf()]
                    for _, k_ in OTs:
                        resv.add(int(k_[2:]))
                    cache_ev = [last_ev(PE), last_ev(ACT), last_ev(DVE)]

                    def ots(b, gp):
                        bank, _ = OTs[b // 8]
                        o = ((b % 8) * 2 + gp) * 32
                        return bank[:, o:o + 32]
                    okeys = [OTs[0][1], OTs[1][1]]
                    op(DVE, lambda h: h.memset(pnew_all[:, :], 0.0), (), ("pnew_all",))
                    for gp in range(2):
                        copy(DVE, qs_sb[:, :, gp, :].rearrange("p b (r i) -> p b r i", r=4),
                             qT[:, gp * 4:gp * 4 + 4, scol0:scol0 + 128].rearrange("p r (b i) -> p b r i", b=16),
                             [K("qT", gp * 4 + r, tl) for r in range(4)], ("qs_sb",))
                    for rb in range(4):
                        bb = rb % 2
                        dma(plq, Kc_bf[bb], ck[l, rb * 4:rb * 4 + 4].rearrange("b p f -> p b f"), writes=(f"Kc{bb}",), extra=[e for e in cache_ev if e])
                        dma(plq, Vc_bf[bb], cv[l, rb * 4:rb * 4 + 4].rearrange("b p f -> p b f"), writes=(f"Vc{bb}",), extra=[e for e in cache_ev if e])
                        bankT, btkey = nb()
                        pe([lambda t, bl=bl, gp=gp: t.transpose(out=bankT[:, (bl * 2 + gp) * 128:(bl * 2 + gp + 1) * 128],
                                                                  in_=Kc_bf[bb][:, bl, gp * 128:(gp + 1) * 128], identity=ident[:])
                            for bl in range(4) for gp in range(2)], (f"Kc{bb}", "c:ident"), (btkey,))
                        copy(ACT, KcT[:, :, :, :], bankT[:, :].rearrange("p (b g k) -> p b g k", b=4, g=2), (btkey,), ("KcT",))
                        sc_bank, sc_key = nf()
                        sn_bank, sn_key = nf()
                        mms = []
                        for bl in range(4):
                            b = rb * 4 + bl
                            for g in range(4):
                                gp, half = g // 2, g % 2
                                rows = slice(half * 64, half * 64 + 64)
                                mms.append(lambda t, bl=bl, b=b, g=g, gp=gp, half=half, rows=rows: t.matmul(
                                    sc_bank[32 * g:32 * g + 32, bl * 128:(bl + 1) * 128],
                                    lhsT=qs_sb[rows, b, gp, :],
                                    rhs=KcT[rows, bl, gp, :], start=True, stop=True, tile_position=(half * 64, 32 * g)))
                                mms.append(lambda t, bl=bl, b=b, g=g, gp=gp, half=half, rows=rows: t.matmul(
                                    sn_bank[32 * g:32 * g + 32, bl * 8:(bl + 1) * 8],
                                    lhsT=qs_sb[rows, b, gp, :],
                                    rhs=kT[rows, gp, sl * 128 + b * 8:sl * 128 + b * 8 + 8], start=True, stop=True, tile_position=(half * 64, 32 * g)))
                        pe(mms, ["KcT", K("kT", sl), "qs_sb"], (sc_key, sn_key))
                        op(DVE, lambda h: h.tensor_tensor(out=ss_sb[:, :, 0:128], in0=sc_bank[:, :].rearrange("p (b k) -> p b k", b=4),
                                                          in1=bias_s[:, :, 0:128], op=ALU.add), (sc_key, "c:bias_s"), ("ss_sb",))
                        op(DVE, lambda h: h.tensor_tensor(out=ss_sb[:, :, 128:136], in0=sn_bank[:, 0:32].rearrange("p (b k) -> p b k", b=4),
                                                          in1=bias_s[:, :, 128:136], op=ALU.add), (sn_key, "c:bias_s"), ("ss_sb",))
                        kk = ("st",)
                        op(DVE, lambda h: h.tensor_reduce(out=st[:, 16:20], in_=ss_sb[:, :, :], axis=AX.X, op=ALU.max), ("ss_sb",) + kk, kk)
                        op(DVE, lambda h: h.tensor_tensor(out=st[:, 16:20], in0=st[:, 16:20], in1=sink_s4[:, :], op=ALU.max), ("sink_s4",) + kk, kk)
                        op(DVE, lambda h: h.tensor_scalar(out=st[:, 20:24], in0=st[:, 16:20], scalar1=-1.0, scalar2=None, op0=ALU.mult), kk, kk)
                        op(DVE, lambda h: h.tensor_tensor(out=st[:, 24:28], in0=sink_s4[:, :], in1=st[:, 20:24], op=ALU.add), ("sink_s4",) + kk, kk)
                        for bl in range(4):
                            op(ACT, lambda h, bl=bl: h.activation(out=ps_sb[:, bl, :], in_=ss_sb[:, bl, :], func=AF.Exp, bias=st[:, 20 + bl:21 + bl],
                                                                   accum_out=st[:, 28 + bl:29 + bl]), ("ss_sb",) + kk, ("ps_sb",) + kk)
                        op(ACT, lambda h: h.activation(out=st[:, 24:28], in_=st[:, 24:28], func=AF.Exp), kk, kk)
                        op(DVE, lambda h: h.tensor_tensor(out=st[:, 32:36], in0=st[:, 28:32], in1=st[:, 24:28], op=ALU.add), kk, kk)
                        op(DVE, lambda h: h.reciprocal(out=st[:, 36:40], in_=st[:, 32:36]), kk, kk)
                        for bl in range(4):
                            op(DVE, lambda h, bl=bl: h.tensor_scalar(out=pns_sb[:, bl, :], in0=ps_sb[:, bl, :], scalar1=st[:, 36 + bl:37 + bl], scalar2=None, op0=ALU.mult),
                               ("ps_sb",) + kk, ("pns_sb",))
                        copy(DVE, pnew_all[:, rb * 32:(rb + 1) * 32].rearrange("p (b k) -> p b k", b=4), pns_sb[:, :, 128:136], ("pns_sb",), ("pnew_all",))
                        bankT2, bt2key = nb()
                        pe([lambda t, bl=bl: t.transpose(out=bankT2[:, bl * 128:(bl + 1) * 128], in_=pns_sb[:, bl, 0:128], identity=ident[:])
                            for bl in range(4)], ("pns_sb", "c:ident"), (bt2key,))
                        copy(ACT, PTc_sb[:, :], bankT2[:, 0:512], (bt2key,), ("PTc_sb",))
                        bankT3, bt3key = nb()
                        pe([lambda t: t.transpose(out=bankT3[:, 0:128], in_=pnew_all[:, :], identity=ident[:])], ("pnew_all", "c:ident"), (bt3key,))
                        copy(ACT, PTn_sb[:, :], bankT3[:, 0:128], (bt3key,), ("PTn_sb",))
                        for bl in range(4):
                            b = rb * 4 + bl
                            op(DVE, lambda h, b=b, bl=bl: h.tensor_scalar(out=PTn_bd[:, bl, :], in0=PTn_sb[:, :], scalar1=bmask[:, b:b + 1], scalar2=None, op0=ALU.mult),
                               ("PTn_sb", "c:bmask"), ("PTn_bd",))
                        mms = []
                        for bl in range(4):
                            b = rb * 4 + bl
                            for g in range(4):
                                gp, half = g // 2, g % 2
                                mms.append(lambda t, bl=bl, b=b, g=g, gp=gp, half=half: t.matmul(
                                    ots(b, gp)[half * 64:half * 64 + 64, :], lhsT=Vc_bf[bb][:, bl, g * 64:(g + 1) * 64],
                                    rhs=PTc_sb[:, bl * 128 + g * 32:bl * 128 + g * 32 + 32], start=True, stop=False, tile_position=(0, half * 64)))
                                mms.append(lambda t, bl=bl, b=b, g=g, gp=gp, half=half: t.matmul(
                                    ots(b, gp)[half * 64:half * 64 + 64, :], lhsT=Vt[:, sl, g * 64:(g + 1) * 64],
                                    rhs=PTn_bd[:, bl, g * 32:g * 32 + 32], start=False, stop=True, tile_position=(0, half * 64)))
                        pe(mms, ("PTc_sb", f"Vc{bb}", "PTn_bd", K("V", sl)), okeys)
                    for hb in range(2):
                        for gp in range(2):
                            bank, bkey = OTs[hb]
                            src = bank[:, :].rearrange("p (b g r i) -> p g r b i", b=8, g=2, r=4)[:, gp]
                            dst = ybT[:, gp * 4:gp * 4 + 4, scol0 + hb * 64:scol0 + hb * 64 + 64].rearrange("p r (b i) -> p r b i", b=8)
                            copy(DVE, dst, src, (bkey,), [K("ybT", gp * 4 + r, tl) for r in range(4)])

                resv.clear()
                copy(DVE, carK[:, l, :, :], kT[:, :, GT * 128:(GT + 1) * 128], [K("kT", GT)], ("carK",))
                copy(DVE, carV[:, l, :], Vt[:, GT, :], [K("V", GT)], ("carV",))

                for _ in dense_gen(1):
                    pass

                r2_ev = [e for e in (last_ev(PE), last_ev(ACT), last_ev(DVE)) if e]
                wdv = w_down[l].rearrange("(kc p) n -> p kc n", p=128)
                dma(plq, wdn[:, 0:11, :], wdv[:, 0:11, :], writes=("wdn",), extra=r2_ev)
                dma(plq, wdn[:, 11:22, :], wdv[:, 11:22, :], writes=("wdn",), extra=r2_ev)

                load_ln(l, 1)
                if G == groups[0] and l == 0:
                    for l_ in range(depth_run):
                        dma(spq, ks[l_, :, 0:120, :], ck[l_, :, 8:128, :], is_out=True)
                        dma(spq, vs[l_, :, 0:120, :], cv[l_, :, 8:128, :], is_out=True)
                wos_ = [ws.slot(b_) for b_ in pl["wo"]]
                for tl in active:
                    for hf in range(2):
                        bank, bkey = nf()
                        pe([lambda t, kc=kc, q2=q2, tl=tl: t.matmul(bank[:, q2 * BW:(q2 + 1) * BW], lhsT=mixT[:, kc, tl * 128:(tl + 1) * 128],
                                                                      rhs=ring[:, wos_[hf * 2 + q2], kc, :], start=(kc == 0), stop=(kc == 7))
                            for q2 in range(2) for kc in range(8)],
                           [f"ring{wos_[hf * 2]}", f"ring{wos_[hf * 2 + 1]}"] + [K("mixT", kc, tl) for kc in range(8)], (bkey,))
                        op(DVE, lambda h, bank=bank, hf=hf, tl=tl: h.scalar_tensor_tensor(out=x_tok[:, tl, hf * 512:(hf + 1) * 512], in0=bank[:, :], scalar=c_res,
                                                                                         in1=x_tok[:, tl, hf * 512:(hf + 1) * 512], op0=ALU.mult, op1=ALU.add),
                           (bkey, xk(tl)), (xk(tl),))
                    ln_stats(tl)
                for b_ in pl["wo"]:
                    ws.done(b_)
                ln_group(active, eps_res, want_xT=True)

                for bi, (gblk, ublk) in enumerate(pl["gu"]):
                    sg, su = ws.slot(gblk), ws.slot(ublk)
                    for cc in range(2):
                        c = bi * 2 + cc
                        for seg in segs:
                            n = len(seg) * 128
                            gb_ = (c * 2 + segs.index(seg)) % 2
                            bank, bkey = ws_job(f"ring{sg}", lambda kc, sg=sg, cc=cc: ring[:, sg, kc, cc * 128:(cc + 1) * 128],
                                                lambda kc, seg=seg: xT[:, kc, cols(seg)], xkeys(seg), n)
                            op(ACT, lambda h, bank=bank, gb_=gb_, seg=seg, n=n: h.activation(out=gtmp[:, gb_, cols(seg)], in_=bank[:, 0:n], func=AF.Silu),
                               (bkey,), (f"gtmp{gb_}",))
                            bank2, bkey2 = ws_job(f"ring{su}", lambda kc, su=su, cc=cc: ring[:, su, kc, cc * 128:(cc + 1) * 128],
                                                  lambda kc, seg=seg: xT[:, kc, cols(seg)], xkeys(seg), n)
                            op(DVE, lambda h, bank2=bank2, gb_=gb_, seg=seg, n=n, c=c: h.tensor_tensor(out=hT[:, c, cols(seg)], in0=bank2[:, 0:n],
                                                                                                  in1=gtmp[:, gb_, cols(seg)], op=ALU.mult),
                               (bkey2, f"gtmp{gb_}"), [K("hT", c, tl) for tl in seg])
                    ws.done(gblk)
                    ws.done(ublk)

                load_ln(l, 2)
                for tl in active:
                    for hf in range(2):
                        bank, bkey = nf()
                        pe([lambda t, kc=kc, tl=tl, hf=hf: t.matmul(bank[:, :], lhsT=hT[:, kc, tl * 128:(tl + 1) * 128], rhs=wdn[:, kc, hf * 512:(hf + 1) * 512],
                                                                      start=(kc == 0), stop=(kc == NHC - 1)) for kc in range(NHC)],
                           ["wdn"] + [K("hT", kc, tl) for kc in range(NHC)], (bkey,))
                        op(DVE, lambda h, bank=bank, hf=hf, tl=tl: h.scalar_tensor_tensor(out=x_tok[:, tl, hf * 512:(hf + 1) * 512], in0=bank[:, :], scalar=c_res,
                                                                                         in1=x_tok[:, tl, hf * 512:(hf + 1) * 512], op0=ALU.mult, op1=ALU.add),
                           (bkey, xk(tl)), (xk(tl),))
                    ln_stats(tl)
                ln_group(active, eps_res, want_xT=(l < depth_run - 1))
                for tl in active:
                    if l < depth_run - 1:
                        pass
                    else:
                        t_ = gt(tl)
                        if t_ == SAMPLE_T:
                            dma(spq, ys, x_tok[:, tl, :], reads=(xk(tl),), is_out=True)
                        elif t_ >= 4:
                            dma(spq, yp[(t_ - 4) * 128:(t_ - 3) * 128, :], x_tok[:, tl, :], reads=(xk(tl),), is_out=True)


        flush_deferred()
        done = {}
        for (s_, v_, k_) in out_events:
            if done.get(k_, (None, 0))[1] < v_:
                done[k_] = (s_, v_)
        for k_, (s_, v_) in done.items():
            nc.sync.wait_ge(s_, v_)
    return nc


def _rel_bucket(dist):
    n = np.maximum(dist, 0)
    max_exact = 16
    nf = np.maximum(n, 1).astype(np.float32)
    large = max_exact + (np.log(nf / np.float32(max_exact)) / np.float32(math.log(128 / max_exact)) * np.float32(32 - max_exact)).astype(np.int32)
    large = np.minimum(large, 31)
    return np.where(n < max_exact, n, large)


_NC_CACHE = {}


def _prep(x_prompt, x_sample, cache_swa_k, cache_swa_v, rel_bias, w_in, ln_v_g, ln_v_b, w_s, b_s, sinks,
          w_pa, w_pb, w_o, ln1_g, ln1_b, w_gate, w_up, w_down, ln2_g, ln2_b):
    f32 = np.float32
    A = lambda a: np.ascontiguousarray(np.asarray(a, dtype=f32))
    x_prompt, x_sample = A(x_prompt), A(x_sample)
    ck_all = A(cache_swa_k).reshape(DEPTH, 128, 128, 256)
    cv_all = A(cache_swa_v).reshape(DEPTH, 128, 128, 256)
    w_in, w_pb = A(w_in), A(w_pb)
    perm = np.zeros(1024, dtype=np.int64)
    for gp in range(2):
        for r in range(4):
            for half in range(2):
                h = 4 * (2 * gp + half) + r
                n0 = (gp * 4 + r) * 128 + half * 64
                perm[n0:n0 + 64] = h * 64 + np.arange(64)
    w_in_p = w_in.copy()
    w_in_p[:, :, O_Q:O_Q + 1024] = w_in[:, :, O_Q + perm]
    w_pb_p = np.ascontiguousarray(w_pb[:, perm, :])
    lnp = np.stack([A(ln_v_g), A(ln_v_b), A(ln1_g), A(ln1_b), A(ln2_g), A(ln2_b)], axis=1)
    w_s = A(w_s)
    wsT_p = np.ascontiguousarray(w_s.transpose(0, 3, 1, 2))
    w8 = w_s[:, :, :8, :8]
    wsT_s = np.ascontiguousarray(np.tile(w8.transpose(0, 3, 1, 2), (1, 16, 1, 16)))
    b_s = A(b_s)
    bs_p = np.ascontiguousarray(b_s.reshape(DEPTH, 1, 1024))
    bs_s = np.ascontiguousarray(np.tile(b_s[:, :, :8], (1, 1, 16)).reshape(DEPTH, 1, 1024))
    sinks = A(sinks)
    sinks_s4 = np.ascontiguousarray(np.repeat(np.repeat(sinks, 8, axis=1)[:, :, None], 4, axis=2))
    rel_bias = A(rel_bias)
    qi = np.arange(128)[:, None]
    kj = np.arange(256)[None, :]
    dist = qi + 128 - kj
    bias_val = np.ascontiguousarray(rel_bias[_rel_bucket(dist)].transpose(0, 2, 1))
    m2 = np.where((dist >= 0) & (dist < 128), 0.0, NEG).astype(f32)
    mask_p = np.ascontiguousarray(np.broadcast_to(m2[:, None, :], (128, 16, 256)))
    qs = np.arange(8)[:, None]
    ksj = np.arange(136)[None, :]
    dist_s = qs + 128 - ksj
    bsv = rel_bias[_rel_bucket(dist_s)]
    bsv = bsv.transpose(2, 0, 1).reshape(128, 136)
    bias_s_val = np.ascontiguousarray(np.broadcast_to(bsv[:, None, :], (128, 4, 136)))
    ms = np.where((dist_s >= 0) & (dist_s < 128), 0.0, NEG).astype(f32)
    ms = np.tile(ms, (16, 1))
    mask_s = np.ascontiguousarray(np.broadcast_to(ms[:, None, :], (128, 4, 136)))
    cst = np.zeros((128, 3, 128), dtype=f32)
    cst[:, 0, :] = np.eye(128, dtype=f32)
    cst[:, 1, :] = (np.arange(128)[:, None] <= np.arange(128)[None, :]).astype(f32)
    sidx = np.arange(128)
    cst[:, 2, :] = ((sidx[:, None] // 8 == sidx[None, :] // 8) & (sidx[:, None] % 8 <= sidx[None, :] % 8)).astype(f32)
    bmask = (sidx[:, None] // 8 == np.arange(16)[None, :]).astype(f32)

    shared = dict(w_in=w_in_p, w_pa=A(w_pa), w_pb=w_pb_p, w_o=A(w_o), w_gate=A(w_gate), w_up=A(w_up), w_down=A(w_down),
                  lnp=lnp, wsT_p=wsT_p, wsT_s=wsT_s, bs_p=bs_p, bs_s=bs_s, sinks=sinks, sinks_s4=sinks_s4,
                  bias_val=bias_val, mask_p=mask_p, bias_s_val=bias_s_val, mask_s=mask_s, cst=cst, bmask=bmask)
    in_maps = []
    for c in range(NCORE):
        bq, half = c // 2, c % 2
        xin = np.zeros((NT * 128, D), dtype=f32)
        if half == 1:
            xin[0:512] = x_prompt[bq, 2048 - 512:2048]
        xin[512:512 + 2048] = x_prompt[bq, half * 2048:(half + 1) * 2048]
        xin[2560:2688] = x_sample[c * 16:(c + 1) * 16].reshape(128, D)
        fm = np.zeros((1, 256), dtype=f32)
        if half == 0:
            fm[0, :128] = NEG
        m = dict(shared)
        m.update(xin=xin, ck=np.ascontiguousarray(ck_all[:, c * 16:(c + 1) * 16]), cv=np.ascontiguousarray(cv_all[:, c * 16:(c + 1) * 16]), fmask=fm)
        in_maps.append(m)
    return in_maps


def kernel(**inputs):
    f32 = np.float32
    in_maps = _prep(**inputs)
    if "nc" not in _NC_CACHE:
        _NC_CACHE["nc"] = build_program()
    nc = _NC_CACHE["nc"]
    res = run_bass_kernel_spmd(nc, in_maps, core_ids=list(range(NCORE)))
    R = res.results
    y_prompt = np.zeros((4, 4096, D), dtype=f32)
    y_sample = np.zeros((128, 8, D), dtype=f32)
    kp = np.zeros((DEPTH, 4, 128, 4, 64), dtype=f32)
    vp = np.zeros((DEPTH, 4, 128, 4, 64), dtype=f32)
    ks = np.zeros((DEPTH, 128, 128, 4, 64), dtype=f32)
    vs = np.zeros((DEPTH, 128, 128, 4, 64), dtype=f32)
    gvo = np.zeros((DEPTH, 128, 8, D), dtype=f32)
    for c in range(NCORE):
        bq, half = c // 2, c % 2
        r = R[c]
        y_prompt[bq, half * 2048:(half + 1) * 2048] = np.asarray(r["yp"])
        y_sample[c * 16:(c + 1) * 16] = np.asarray(r["ys"]).reshape(16, 8, D)
        if half == 1:
            kp[:, bq] = np.asarray(r["kp"]).reshape(DEPTH, 128, 4, 64)
            vp[:, bq] = np.asarray(r["vp"]).reshape(DEPTH, 128, 4, 64)
        ks[:, c * 16:(c + 1) * 16] = np.asarray(r["ks"]).reshape(DEPTH, 16, 128, 4, 64)
        vs[:, c * 16:(c + 1) * 16] = np.asarray(r["vs"]).reshape(DEPTH, 16, 128, 4, 64)
        gvo[:, c * 16:(c + 1) * 16] = np.asarray(r["gv"]).reshape(DEPTH, 16, 8, D)
    return (y_prompt, y_sample, kp, vp, ks, vs, gvo)
```

```python
import math
import os
import numpy as np
import concourse.bass as bass
import concourse.mybir as mybir
from concourse.bass_utils import run_bass_kernel_spmd

F32 = mybir.dt.float32
BF16 = mybir.dt.bfloat16
I32 = mybir.dt.int32
AF = mybir.ActivationFunctionType
ALU = mybir.AluOpType
AX = mybir.AxisListType

D = 1024
DEPTH = 4
NCORE = 8
NT = 21
GT = 7
NG = 3
T = GT * 128
DFF = 2816
NHC = DFF // 128
INW = 5632
O_U, O_V, O_Q, O_K, O_G = 0, 1024, 2048, 3072, 3584
ALPHA = (2 * DEPTH) ** 0.25
LN_EPS = 1e-5
NEG = -1e30
RING = 6
BW = 256
SAMPLE_T = 20
LASTP_T = 19
FIRST_T = 4


class Eng:
    def __init__(self, name, h, sem):
        self.name, self.h, self.sem = name, h, sem
        self.cnt = 0
        self.waited = {}


class DmaQ:
    def __init__(self, eng, sems, tag):
        self.eng, self.sems, self.tag = eng, sems, tag
        self.k = 0


class Trk:
    def __init__(self):
        self.lw = {}
        self.rd = {}

    def deps(self, reads, writes, extra=()):
        d = {}

        def add(ev):
            if ev is None:
                return
            sem, val, key = ev
            if d.get(key, (None, 0))[1] < val:
                d[key] = (sem, val)

        for k in reads:
            add(self.lw.get(k))
        for k in writes:
            add(self.lw.get(k))
            for ev in self.rd.get(k, {}).values():
                add(ev)
        for ev in extra:
            add(ev)
        return d

    def wait(self, eng, d):
        for key, (sem, val) in d.items():
            if eng.waited.get(key, 0) < val:
                eng.h.wait_ge(sem, val)
                eng.waited[key] = val

    def commit(self, ev, reads, writes):
        for k in reads:
            if k.startswith("c:"):
                continue
            r = self.rd.setdefault(k, {})
            if r.get(ev[2], (None, 0, None))[1] < ev[1]:
                r[ev[2]] = ev
        for k in writes:
            self.lw[k] = ev
            self.rd[k] = {}


def build_program(groups=(0, 1, 2), depth_run=DEPTH):
    nc = bass.Bass("TRN2", target_bir_lowering=False)
    dram_in = {}

    def din(name, shape, dt=F32):
        dram_in[name] = nc.dram_tensor(name, list(shape), dt, kind="ExternalInput").ap()
        return dram_in[name]

    def dout(name, shape):
        return nc.dram_tensor(name, list(shape), F32, kind="ExternalOutput").ap()

    xin = din("xin", [NT * 128, D])
    ck = din("ck", [DEPTH, 16, 128, 256])
    cv = din("cv", [DEPTH, 16, 128, 256])
    w_in = din("w_in", [DEPTH, D, INW])
    w_pa = din("w_pa", [DEPTH, D, D])
    w_pb = din("w_pb", [DEPTH, D, D])
    w_o = din("w_o", [DEPTH, D, D])
    w_gate = din("w_gate", [DEPTH, D, DFF])
    w_up = din("w_up", [DEPTH, D, DFF])
    w_down = din("w_down", [DEPTH, DFF, D])
    lnp = din("lnp", [DEPTH, 6, D])
    wsT_p = din("wsT_p", [DEPTH, 128, 8, 128])
    wsT_s = din("wsT_s", [DEPTH, 128, 8, 128])
    bs_p = din("bs_p", [DEPTH, 1, D])
    bs_s = din("bs_s", [DEPTH, 1, D])
    sinks = din("sinks", [DEPTH, 16])
    sinks_s4 = din("sinks_s4", [DEPTH, 128, 4])
    bias_val = din("bias_val", [128, 16, 256])
    mask_p = din("mask_p", [128, 16, 256])
    bias_s_val = din("bias_s_val", [128, 4, 136])
    mask_s = din("mask_s", [128, 4, 136])
    fmask = din("fmask", [1, 256])
    cst = din("cst", [128, 3, 128])
    bmask_d = din("bmask", [128, 16])

    yp = dout("yp", [16 * 128, D])
    ys = dout("ys", [128, D])
    kp = dout("kp", [DEPTH, 128, 256])
    vp = dout("vp", [DEPTH, 128, 256])
    ks = dout("ks", [DEPTH, 16, 128, 256])
    vs = dout("vs", [DEPTH, 16, 128, 256])
    gv = dout("gv", [DEPTH, 128, D])

    from contextlib import ExitStack
    with ExitStack() as es:
        def sb(name, shape, dt):
            return es.enter_context(nc.sbuf_tensor(name, list(shape), dt))

        x_tok = sb("x_tok", [128, GT, D], F32)
        xT = sb("xT", [128, 8, T], BF16)
        R1 = sb("R1", [128, 3 * 8 * T], BF16)
        R2 = sb("R2", [128, NHC * D], BF16)
        bias = sb("bias", [128, 16, 256], F32)
        bias_s = sb("bias_s", [128, 4, 136], F32)
        ring = sb("ring", [128, RING, 8, BW], BF16)
        lnbuf = sb("lnbuf", [128, 2, D], F32)
        tmpv = sb("tmpv", [128, D], F32)
        gtmp = sb("gtmp", [128, 2, T], BF16)
        bstb = gtmp[0:33, :, :].rearrange("p a t -> p (a t)")[:, 0:D]
        scr2k = sb("scr2k", [128, 512], F32)
        ftmp = scr2k
        kvout = scr2k
        wsb_p = sb("wsb_p", [128, 8, 128], BF16)
        wsb_s = sb("wsb_s", [128, 8, 128], BF16)
        bsstg = sb("bsstg", [33, D], F32)
        bsrow = sb("bsrow", [33, 2, D], BF16)
        carK = sb("carK", [128, DEPTH, 2, 128], BF16)
        carV = sb("carV", [128, DEPTH, 256], BF16)
        cstf = sb("cstf", [128, 3, 128], F32)
        ident = sb("ident", [128, 128], BF16)
        ones_b = sb("ones_b", [33, 128], BF16)
        fm_b = sb("fm_b", [1, 256], BF16)
        bmask = sb("bmask_sb", [128, 16], F32)
        sink_bc = sb("sink_bc", [128, 16], F32)
        sink_s4 = sb("sink_s4", [128, 4], F32)
        nsink_bc = sb("nsink_bc", [128, 16], F32)
        st = sb("st", [128, 64], F32)
        stA = sb("stA", [128, 2, 32], F32)
        lnst = sb("lnst", [128, 64], F32)
        mv2 = sb("mv2", [128, GT, 2], F32)
        xbf = scr2k[:, :].bitcast(BF16)
        xbfs = [(xbf, "scr2k"), (tmpv[:, 0:512].bitcast(BF16), "tmpv")]

        uT = R1[:, 0:8 * T].rearrange("p (c t) -> p c t", c=8)
        va = R1[:, 8 * T:16 * T].rearrange("p (t f) -> p t f", t=GT)
        mixT = R1[:, 8 * T:16 * T].rearrange("p (c t) -> p c t", c=8)
        qT = R1[:, 16 * T:24 * T].rearrange("p (c t) -> p c t", c=8)
        hT = R1[:, 0:NHC * T].rearrange("p (c t) -> p c t", c=NHC)
        o2 = 0
        ybT = R2[:, o2:o2 + 8 * T].rearrange("p (c t) -> p c t", c=8); o2 += 8 * T
        kT = R2[:, o2:o2 + 2 * 8 * 128].rearrange("p (c t) -> p c t", c=2); o2 += 2 * 8 * 128
        Vt = R2[:, o2:o2 + 8 * 256].rearrange("p (s f) -> p s f", s=8); o2 += 8 * 256
        A0 = o2

        def r2v(off, n):
            return R2[:, A0 + off:A0 + off + n]
        s_sb = [r2v(i * 2048, 2048).bitcast(F32).rearrange("p (r k) -> p r k", r=4) for i in range(2)]
        p_sb = [r2v(4096 + i * 1024, 1024).rearrange("p (r k) -> p r k", r=4) for i in range(2)]
        pn_sb = [r2v(6144 + i * 1024, 1024).rearrange("p (r k) -> p r k", r=4) for i in range(2)]
        PT_sb = [r2v(8192 + i * 1024, 1024) for i in range(2)]
        Kc_bf = [r2v(i * 1024, 1024).rearrange("p (b f) -> p b f", b=4) for i in range(2)]
        Vc_bf = [r2v(2048 + i * 1024, 1024).rearrange("p (b f) -> p b f", b=4) for i in range(2)]
        KcT = r2v(4096, 1024).rearrange("p (b g k) -> p b g k", b=4, g=2)
        ss_sb = r2v(5120, 2 * 544).bitcast(F32).rearrange("p (b k) -> p b k", b=4)
        ps_sb = r2v(6208, 544).rearrange("p (b k) -> p b k", b=4)
        pns_sb = r2v(6752, 544).rearrange("p (b k) -> p b k", b=4)
        PTc_sb = r2v(7296, 512)
        pnew_all = r2v(7808, 128)
        PTn_sb = r2v(7936, 128)
        PTn_bd = r2v(8064, 2048).rearrange("p (b k) -> p b k", b=16)
        qs_sb = r2v(10112, 1024).rearrange("p (b g k) -> p b g k", b=16, g=2)
        wdn = R2[:, 0:NHC * D].rearrange("p (c n) -> p c n", c=NHC)

        pf = [es.enter_context(nc.psum_tensor(f"pf{i}", [128, 512], F32)) for i in range(6)]
        pbk = [es.enter_context(nc.psum_tensor(f"pb{i}", [128, 1024], BF16)) for i in range(2)]

        def sem(name):
            return es.enter_context(nc.semaphore(name))

        PE = Eng("pe", nc.tensor, sem("s_pe"))
        ACT = Eng("act", nc.scalar, sem("s_act"))
        DVE = Eng("dve", nc.vector, sem("s_dve"))
        SP = Eng("sp", nc.sync, sem("s_sp"))
        POOL = Eng("pool", nc.gpsimd, sem("s_pool"))
        NDS = 8
        spq = DmaQ(SP, [sem(f"d_sp{i}") for i in range(NDS)], "dsp")
        plq = DmaQ(POOL, [sem(f"d_pl{i}") for i in range(NDS)], "dpl")
        trk = Trk()
        OQ = plq if os.environ.get('OQ_POOL') else spq
        out_events = []

        def excl(reads, writes):
            ps = [k for k in reads if k.startswith("pf") or k.startswith("pb")]
            if ps:
                reads = [k for k in reads if k not in ps]
                writes = list(writes) + ps
            return reads, writes

        def op(eng, fn, reads=(), writes=(), extra=()):
            reads, writes = excl(reads, writes)
            d = trk.deps(reads, writes, extra)
            trk.wait(eng, d)
            ins = fn(eng.h)
            eng.cnt += 1
            ins.then_inc(eng.sem, 1)
            ev = (eng.sem, eng.cnt, eng.name)
            trk.commit(ev, reads, writes)
            return ev

        def pe(mms, reads=(), writes=(), extra=()):
            d = trk.deps(reads, writes, extra)
            trk.wait(PE, d)
            ins = None
            for m in mms:
                ins = m(nc.tensor)
            PE.cnt += 1
            ins.then_inc(PE.sem, 1)
            ev = (PE.sem, PE.cnt, "pe")
            trk.commit(ev, reads, writes)
            return ev

        def dma(q, out, in_, reads=(), writes=(), extra=(), is_out=False):
            eng = q.eng
            d = trk.deps(reads, writes, extra)
            trk.wait(eng, d)
            i = q.k % NDS
            n = q.k // NDS
            if n > 0:
                key = f"{q.tag}{i}"
                if eng.waited.get(key, 0) < 16 * n:
                    eng.h.wait_ge(q.sems[i], 16 * n)
                    eng.waited[key] = 16 * n
            ins = eng.h.dma_start(out=out, in_=in_)
            ins.then_inc(q.sems[i], 16)
            ev = (q.sems[i], 16 * (n + 1), f"{q.tag}{i}")
            q.k += 1
            trk.commit(ev, reads, writes)
            if is_out:
                out_events.append(ev)
            return ev

        def last_ev(eng):
            return (eng.sem, eng.cnt, eng.name) if eng.cnt > 0 else None

        pfi = [0]
        pbi = [0]

        resv = set()

        def nf():
            while True:
                i = pfi[0] % 6
                pfi[0] += 1
                if i not in resv:
                    return pf[i], f"pf{i}"

        def nb():
            i = pbi[0] % 2
            pbi[0] += 1
            return pbk[i], f"pb{i}"

        alt = [0]

        def evac_eng():
            alt[0] ^= 1
            return ACT if alt[0] else DVE

        def copy(eng, out, in_, reads, writes, scale=None):
            if eng is ACT:
                if scale is None:
                    return op(ACT, lambda h: h.copy(out=out, in_=in_), reads, writes)
                return op(ACT, lambda h: h.activation(out=out, in_=in_, func=AF.Copy, scale=scale), reads, writes)
            if scale is None:
                return op(DVE, lambda h: h.tensor_copy(out=out, in_=in_), reads, writes)
            return op(DVE, lambda h: h.tensor_scalar(out=out, in0=in_, scalar1=scale, scalar2=None, op0=ALU.mult), reads, writes)

        class WStream:
            def __init__(self):
                self.blocks = []
                self.issued = 0

            def add(self, ap, nk, ncols):
                self.blocks.append((ap, nk, ncols))
                return len(self.blocks) - 1

            def issue_upto(self, n):
                while self.issued < min(n, len(self.blocks)):
                    i = self.issued
                    ap, nk, ncols = self.blocks[i]
                    s = i % RING
                    dma(plq, ring[:, s, 0:nk, 0:ncols], ap, reads=(), writes=(f"ring{s}",))
                    self.issued += 1

            def slot(self, i):
                self.issue_upto(i + 1)
                return i % RING

            def done(self, i):
                self.issue_upto(i + RING + 1)

        ws = WStream()

        def wview(w, l, c0, ncols, k0=0, nk=8):
            return w[l].rearrange("(kc p) n -> p kc n", p=128)[:, k0:k0 + nk, c0:c0 + ncols]

        plan = {}
        for G in groups:
            for l in range(depth_run):
                b = {}
                b["kv"] = [ws.add(wview(w_in, l, O_K + i * BW, BW), 8, BW) for i in range(2)]
                b["q"] = [ws.add(wview(w_in, l, O_Q + i * BW, BW), 8, BW) for i in range(4)]
                b["u"] = [ws.add(wview(w_in, l, O_U + i * BW, BW), 8, BW) for i in range(2)]
                b["v"] = [ws.add(wview(w_in, l, O_V + i * BW, BW), 8, BW) for i in range(4)]
                b["u"] += [ws.add(wview(w_in, l, O_U + i * BW, BW), 8, BW) for i in range(2, 4)]
                b["ga_pa"] = []
                for i in range(4):
                    b["ga_pa"].append((ws.add(wview(w_in, l, O_G + i * BW, BW), 8, BW),
                                       ws.add(wview(w_pa, l, i * BW, BW), 8, BW)))
                b["gb_pb"] = []
                for i in range(4):
                    b["gb_pb"].append((ws.add(wview(w_in, l, O_G + D + i * BW, BW), 8, BW),
                                       ws.add(wview(w_pb, l, i * BW, BW), 8, BW)))
                b["wo"] = [ws.add(wview(w_o, l, i * BW, BW), 8, BW) for i in range(4)]
                b["gu"] = []
                for i in range(11):
                    b["gu"].append((ws.add(wview(w_gate, l, i * BW, BW), 8, BW),
                                    ws.add(wview(w_up, l, i * BW, BW), 8, BW)))
                plan[(G, l)] = b

        ws.issue_upto(RING)
        def K(name, *idx):
            return name + ":" + ":".join(str(i) for i in idx)

        dma(spq, cstf[:], cst, writes=("c:cstf",))
        copy(DVE, ident[:], cstf[:, 0, :], ("c:cstf",), ("c:ident",))
        op(DVE, lambda h: h.memset(ones_b[:], 1.0), (), ("c:ones",))
        op(DVE, lambda h: h.memset(bsrow[:], 0.0), (), ("bsrow",))
        op(DVE, lambda h: h.memset(R2[:, 8 * T:A0], 0.0), (), ("kTV",))
        op(DVE, lambda h: h.memset(carK[:], 0.0), (), ("carK",))
        op(DVE, lambda h: h.memset(carV[:], 0.0), (), ("carV",))
        op(DVE, lambda h: h.memset(mv2[:], 1.0), (), ("lnst",))
        xk = lambda tl: K("x", tl)
        for tl in range(GT):
            t_ = groups[0] * GT + tl
            dma(spq, x_tok[:, tl, :], xin[t_ * 128:(t_ + 1) * 128, :], writes=(xk(tl),))
        dma(spq, bsstg[0:1, 0:256], fmask, writes=("bsstg",))
        copy(DVE, fm_b[:], bsstg[0:1, 0:256], ("bsstg",), ("c:fm_b",))
        dma(spq, bmask[:], bmask_d, writes=("c:bmask",))
        dma(spq, bias[:], bias_val, writes=("bias",))
        for hh in range(0, 16, 4):
            dma(spq, lnbuf[:, 0, :].rearrange("p (h k) -> p h k", h=4), mask_p[:, hh:hh + 4, :], writes=("lnbuf",))
            op(DVE, lambda h, hh=hh: h.tensor_tensor(out=bias[:, hh:hh + 4, :], in0=bias[:, hh:hh + 4, :],
                                                      in1=lnbuf[:, 0, :].rearrange("p (h k) -> p h k", h=4), op=ALU.add),
               ("bias", "lnbuf"), ("bias",))
        dma(spq, bias_s[:], bias_s_val, writes=("bias_s",))
        dma(spq, lnbuf[:, 1, 0:544].rearrange("p (b k) -> p b k", b=4), mask_s, writes=("lnbuf",))
        op(DVE, lambda h: h.tensor_tensor(out=bias_s[:], in0=bias_s[:], in1=lnbuf[:, 1, 0:544].rearrange("p (b k) -> p b k", b=4), op=ALU.add),
           ("bias_s", "lnbuf"), ("bias_s",))
        trk.lw["c:bias"] = trk.lw["bias"]
        trk.lw["c:bias_s"] = trk.lw["bias_s"]

        def rsqrt_col(var_ap, eps, out_ap):
            kk = ("st",)
            op(DVE, lambda h: h.tensor_scalar(out=st[:, 40:41], in0=var_ap, scalar1=eps, scalar2=None, op0=ALU.add), kk, kk)
            op(DVE, lambda h: h.tensor_copy(out=st[:, 41:42], in_=st[:, 40:41].bitcast(I32)), kk, kk)
            op(DVE, lambda h: h.tensor_scalar(out=st[:, 42:43], in0=st[:, 41:42], scalar1=-0.5, scalar2=float(0x5f3759df), op0=ALU.mult, op1=ALU.add), kk, kk)
            op(DVE, lambda h: h.tensor_copy(out=out_ap.bitcast(I32), in_=st[:, 42:43]), kk, kk)
            op(DVE, lambda h: h.tensor_scalar(out=st[:, 44:45], in0=st[:, 40:41], scalar1=-0.5, scalar2=None, op0=ALU.mult), kk, kk)
            for _ in range(3):
                op(DVE, lambda h: h.scalar_tensor_tensor(out=st[:, 43:44], in0=out_ap, scalar=out_ap, in1=st[:, 44:45], op0=ALU.mult, op1=ALU.mult), kk, kk)
                op(DVE, lambda h: h.scalar_tensor_tensor(out=out_ap, in0=st[:, 43:44], scalar=1.5, in1=out_ap, op0=ALU.add, op1=ALU.mult), kk, kk)

        def layer_norm(src, skey, eps, out2=None, out2_keys=()):
            kk = ("st",)
            op(DVE, lambda h: h.bn_stats(out=st[:, 0:6], in_=src[:, 0:512]), (skey,) + kk, kk)
            op(DVE, lambda h: h.bn_stats(out=st[:, 6:12], in_=src[:, 512:1024]), (skey,) + kk, kk)
            op(DVE, lambda h: h.bn_aggr(out=st[:, 12:14], in_=st[:, 0:12]), kk, kk)
            rsqrt_col(st[:, 13:14], eps, st[:, 14:15])
            op(DVE, lambda h: h.scalar_tensor_tensor(out=src, in0=src, scalar=st[:, 12:13], in1=lnbuf[:, 0, :],
                                                      op0=ALU.subtract, op1=ALU.mult), (skey, "lnbuf") + kk, (skey,))
            dst = src if out2 is None else out2
            op(DVE, lambda h: h.scalar_tensor_tensor(out=dst, in0=src, scalar=st[:, 14:15], in1=lnbuf[:, 1, :],
                                                      op0=ALU.mult, op1=ALU.add), (skey, "lnbuf") + kk, (skey,) if out2 is None else tuple(out2_keys))

        def ln_stats(tl):
            kk = ("lnst",)
            src = x_tok[:, tl, :]
            op(DVE, lambda h: h.bn_stats(out=lnst[:, 0:6], in_=src[:, 0:512]), (xk(tl),) + kk, kk)
            op(DVE, lambda h: h.bn_stats(out=lnst[:, 6:12], in_=src[:, 512:1024]), (xk(tl),) + kk, kk)
            op(DVE, lambda h: h.bn_aggr(out=mv2[:, tl, :], in_=lnst[:, 0:12]), kk, kk)

        def ln_group(tls, eps, want_xT=True):
            kk = ("lnst",)
            var = mv2[:, :, 1]
            vv, ti, tf, yy, tt = lnst[:, 16:23], lnst[:, 24:31], lnst[:, 32:39], lnst[:, 40:47], lnst[:, 48:55]
            op(DVE, lambda h: h.tensor_scalar(out=vv, in0=var, scalar1=eps, scalar2=None, op0=ALU.add), kk, kk)
            op(DVE, lambda h: h.tensor_copy(out=ti, in_=vv.bitcast(I32)), kk, kk)
            op(DVE, lambda h: h.tensor_scalar(out=tf, in0=ti, scalar1=-0.5, scalar2=float(0x5f3759df), op0=ALU.mult, op1=ALU.add), kk, kk)
            op(DVE, lambda h: h.tensor_copy(out=yy.bitcast(I32), in_=tf), kk, kk)
            op(DVE, lambda h: h.tensor_scalar(out=vv, in0=vv, scalar1=-0.5, scalar2=None, op0=ALU.mult), kk, kk)
            for _ in range(3):
                op(DVE, lambda h: h.tensor_tensor(out=tt, in0=yy, in1=yy, op=ALU.mult), kk, kk)
                op(DVE, lambda h: h.tensor_tensor(out=tt, in0=tt, in1=vv, op=ALU.mult), kk, kk)
                op(DVE, lambda h: h.scalar_tensor_tensor(out=yy, in0=tt, scalar=1.5, in1=yy, op0=ALU.add, op1=ALU.mult), kk, kk)
            for tl in tls:
                src = x_tok[:, tl, :]
                op(DVE, lambda h, src=src, tl=tl: h.scalar_tensor_tensor(out=src, in0=src, scalar=mv2[:, tl, 0:1], in1=lnbuf[:, 0, :],
                                                                          op0=ALU.subtract, op1=ALU.mult), (xk(tl), "lnbuf") + kk, (xk(tl),))
                if want_xT:
                    i = xbi[0] % 2
                    xbi[0] += 1
                    xb, xkey = xbfs[i]
                    op(DVE, lambda h, src=src, tl=tl, xb=xb: h.scalar_tensor_tensor(out=xb, in0=src, scalar=lnst[:, 40 + tl:41 + tl], in1=lnbuf[:, 1, :],
                                                                                 op0=ALU.mult, op1=ALU.add), (xk(tl), "lnbuf") + kk, (xkey,))
                    bank, bkey = nb()
                    pe([lambda t, c=c, xb=xb: t.transpose(out=bank[:, c * 128:(c + 1) * 128], in_=xb[:, c * 128:(c + 1) * 128], identity=ident[:])
                        for c in range(8)], (xkey, "c:ident"), (bkey,))
                    copy(ACT, xT[:, :, tl * 128:(tl + 1) * 128], bank[:, :].rearrange("p (c t) -> p c t", c=8), (bkey,), [K("xT", tl)])

                def passB(src=src, tl=tl):
                    op(DVE, lambda h: h.scalar_tensor_tensor(out=src, in0=src, scalar=lnst[:, 40 + tl:41 + tl], in1=lnbuf[:, 1, :],
                                                              op0=ALU.mult, op1=ALU.add), (xk(tl), "lnbuf") + kk, (xk(tl),))
                if want_xT:
                    deferred.append(passB)
                else:
                    passB()

        deferred = []
        xbi = [0]

        def run_deferred(n=1):
            for _ in range(n):
                if deferred:
                    deferred.pop(0)()

        def flush_deferred():
            run_deferred(len(deferred))

        def load_ln(l, which):
            flush_deferred()
            dma(spq, lnbuf[:, 0, :], lnp[l, 2 * which:2 * which + 1, :].partition_broadcast(128), writes=("lnbuf",))
            dma(spq, lnbuf[:, 1, :], lnp[l, 2 * which + 1:2 * which + 2, :].partition_broadcast(128), writes=("lnbuf",))

        def make_xT(tl):
            copy(ACT, xbf, x_tok[:, tl, :], (xk(tl),), ("scr2k",))
            bank, bkey = nb()
            pe([lambda t, c=c: t.transpose(out=bank[:, c * 128:(c + 1) * 128], in_=xbf[:, c * 128:(c + 1) * 128], identity=ident[:])
                for c in range(8)], ("scr2k", "c:ident"), (bkey,))
            copy(evac_eng(), xT[:, :, tl * 128:(tl + 1) * 128], bank[:, :].rearrange("p (c t) -> p c t", c=8),
                 (bkey,), [K("xT", tl)])

        def segs_of(active):
            return [active[i:i + 4] for i in range(0, len(active), 4)]

        def cols(seg):
            return slice(seg[0] * 128, (seg[-1] + 1) * 128)

        def ws_job(slot_key, lhs_fn, rhs, rkeys, ncol):
            bank, bkey = nf()
            pe([lambda t, kc=kc: t.matmul(bank[:, 0:ncol], lhsT=lhs_fn(kc), rhs=rhs(kc), start=(kc == 0), stop=(kc == 7))
                for kc in range(8)], [slot_key] + list(rkeys), (bkey,))
            wsj[0] += 1
            if wsj[0] % 3 == 0:
                run_deferred(1)
            return bank, bkey

        wsj = [0]

        c_res = 1.0 / ALPHA
        eps_res = LN_EPS / (ALPHA * ALPHA)

        for G in groups:
            tiles_g = [G * GT + i for i in range(GT)]
            for l in range(depth_run):
                pl = plan[(G, l)]
                active = list(range(l, GT)) if G == 0 else list(range(GT))
                segs = segs_of(active)
                has_sample = (G == 2) and not os.environ.get('NO_SAMPLE')
                gt = lambda tl: G * GT + tl

                dma(spq, sink_bc[:], sinks[l:l + 1, :].partition_broadcast(128), writes=("sink_bc",))
                op(DVE, lambda h: h.tensor_scalar(out=nsink_bc[:], in0=sink_bc[:], scalar1=-1.0, scalar2=None, op0=ALU.mult), ("sink_bc",), ("sink_bc",))
                dma(spq, tmpv[:].rearrange("p (g t) -> p g t", g=8), wsT_p[l], writes=("tmpv",))
                for gi in range(8):
                    op(DVE, lambda h, gi=gi: h.tensor_tensor(out=wsb_p[:, gi, :], in0=tmpv[:, gi * 128:(gi + 1) * 128], in1=cstf[:, 1, :], op=ALU.mult),
                       ("tmpv", "c:cstf"), ("wsb_p",))
                def bias_rows(src, bi_):
                    dma(spq, bsstg[0:1, :], src, writes=("bsstg",))
                    dma(spq, bsstg[32:33, :], src, writes=("bsstg",))
                    op(DVE, lambda h: h.tensor_copy(out=bsrow[0:1, bi_, :], in_=bsstg[0:1, :]), ("bsstg",), ("bsrow",))
                    op(DVE, lambda h: h.tensor_copy(out=bstb[32:33, :], in_=bsstg[32:33, :]), ("bsstg",), ("gtmp0", "gtmp1"))
                    op(DVE, lambda h: h.tensor_tensor(out=bsstg[32:33, :], in0=bsstg[32:33, :], in1=bstb[32:33, :], op=ALU.subtract),
                       ("bsstg", "gtmp0", "gtmp1"), ("bsstg",))
                    op(DVE, lambda h: h.tensor_copy(out=bsrow[32:33, bi_, :], in_=bsstg[32:33, :]), ("bsstg",), ("bsrow",))
                bias_rows(bs_p[l], 0)
                if has_sample:
                    dma(spq, sink_s4[:], sinks_s4[l], writes=("sink_s4",))
                    dma(spq, tmpv[:].rearrange("p (g t) -> p g t", g=8), wsT_s[l], writes=("tmpv",))
                    for gi in range(8):
                        op(DVE, lambda h, gi=gi: h.tensor_tensor(out=wsb_s[:, gi, :], in0=tmpv[:, gi * 128:(gi + 1) * 128], in1=cstf[:, 2, :], op=ALU.mult),
                           ("tmpv", "c:cstf"), ("wsb_s",))
                    bias_rows(bs_s[l], 1)

                if l == 0:
                    if G != groups[0]:
                        for tl in active:
                            dma(spq, x_tok[:, tl, :], xin[gt(tl) * 128:(gt(tl) + 1) * 128, :], writes=(xk(tl),))
                    for tl in active:
                        make_xT(tl)
                    if os.environ.get('TEST_GV0'):
                        if os.environ.get('TEST_GV0') == '1':
                            dma(spq, gv[1], x_tok[:, active[0], :], reads=(xk(active[0]),), is_out=True)
                        elif os.environ.get('TEST_GV0') == '2':
                            for q_ in range(4):
                                dma(spq, gv[1, :, q_ * 256:(q_ + 1) * 256], x_tok[:, active[0], q_ * 256:(q_ + 1) * 256], reads=(xk(active[0]),), is_out=True)
                        elif os.environ.get('TEST_GV0') == '3':
                            dma(spq, yp[0:128, :], x_tok[:, active[0], :], reads=(xk(active[0]),), is_out=True)
                        elif os.environ.get('TEST_GV0') == '4':
                            dma(spq, gv[1, :, 0:512], x_tok[:, active[0], 0:512], reads=(xk(active[0]),), is_out=True)
                    if os.environ.get('TEST_KP0'):
                        dma(spq, kp[1], x_tok[:, active[0], 0:256], reads=(xk(active[0]),), is_out=True)

                xkeys = lambda seg: [K("xT", tl) for tl in seg]
                active_kv, segs_kv = active, segs
                if G == 0:
                    active = active[1:]
                    segs = segs_of(active)

                copy(DVE, kT[:, :, 0:128], carK[:, l, :, :], ("carK",), [K("kT", 0)])
                copy(DVE, Vt[:, 0, :], carV[:, l, :], ("carV",), [K("V", 0)])
                s = ws.slot(pl["kv"][0])
                s_v = ws.slot(pl["kv"][1])
                for tl in active_kv:
                    bank, bkey = nf()
                    pe([lambda t, kc=kc, tl=tl, q2=q2: t.matmul(bank[:, q2 * BW:(q2 + 1) * BW], lhsT=xT[:, kc, tl * 128:(tl + 1) * 128],
                                                                 rhs=ring[:, (s, s_v)[q2], kc, :], start=(kc == 0), stop=(kc == 7))
                        for q2 in range(2) for kc in range(8)],
                       [f"ring{s}", f"ring{s_v}", K("xT", tl)], (bkey,))
                    copy(DVE, Vt[:, tl + 1, :], bank[:, 256:512], (bkey,), [K("V", tl + 1)])
                    if gt(tl) in (LASTP_T, SAMPLE_T) and not os.environ.get('NO_KV'):
                        copy(DVE, kvout[:], bank[:, :], (bkey,), ("scr2k",))
                        if gt(tl) == LASTP_T:
                            dma(OQ, kp[l], kvout[:, 0:256], reads=("scr2k",), is_out=True)
                            dma(OQ, vp[l], kvout[:, 256:512], reads=("scr2k",), is_out=True)
                        elif not os.environ.get('SKIP_KSNEW'):
                            for b_ in range(16):
                                dma(spq, ks[l, b_, 120:128, :], kvout[b_ * 8:(b_ + 1) * 8, 0:256], reads=("scr2k",), is_out=True)
                                dma(spq, vs[l, b_, 120:128, :], kvout[b_ * 8:(b_ + 1) * 8, 256:512], reads=("scr2k",), is_out=True)
                for c in range(2):
                    for seg in segs_kv:
                        n = len(seg) * 128
                        bank, bkey = ws_job(f"ring{s}", lambda kc, s=s, c=c: ring[:, s, kc, c * 128:(c + 1) * 128],
                                            lambda kc, seg=seg: xT[:, kc, cols(seg)], xkeys(seg), n)
                        copy(evac_eng(), kT[:, c, (seg[0] + 1) * 128:(seg[-1] + 2) * 128], bank[:, 0:n], (bkey,), [K("kT", tl + 1) for tl in seg])
                ws.done(pl["kv"][0])
                ws.done(pl["kv"][1])
                for bi, blk in enumerate(pl["q"]):
                    s = ws.slot(blk)
                    for cc in range(2):
                        c = bi * 2 + cc
                        for seg in segs:
                            n = len(seg) * 128
                            bank, bkey = ws_job(f"ring{s}", lambda kc, s=s, cc=cc: ring[:, s, kc, cc * 128:(cc + 1) * 128],
                                                lambda kc, seg=seg: xT[:, kc, cols(seg)], xkeys(seg), n)
                            copy(ACT, qT[:, c, cols(seg)], bank[:, 0:n], (bkey,), [K("qT", c, tl) for tl in seg], scale=0.125)
                    ws.done(blk)

                def stageA(tl, g, bi_):
                    sl = tl + 1
                    tcols = slice(tl * 128, (tl + 1) * 128)
                    gp, half = g // 2, g % 2
                    rows = slice(half * 64, half * 64 + 64)
                    sbuf_s, sbuf_p, sbuf_pn = s_sb[bi_], p_sb[bi_], pn_sb[bi_]
                    sk, pk, pnk = f"s_sb{bi_}", f"p_sb{bi_}", f"pn_sb{bi_}"
                    sa = stA[:, bi_, :]
                    kk = (f"stA{bi_}",)
                    first = (gt(tl) == FIRST_T)
                    banks = []
                    for rp in range(2):
                        bank, bkey = nf()
                        mms = []
                        for rr in range(2):
                            r = rp * 2 + rr
                            mms.append(lambda t, r=r, rr=rr, bank=bank: t.matmul(bank[:, rr * 256:(rr + 1) * 256], lhsT=qT[rows, gp * 4 + r, tcols],
                                                                                  rhs=kT[rows, gp, (sl - 1) * 128:(sl + 1) * 128], start=True, stop=not first))
                            if first:
                                mms.append(lambda t, rr=rr, bank=bank: t.matmul(bank[:, rr * 256:(rr + 1) * 256], lhsT=ones_b[0:1, :], rhs=fm_b[0:1, :],
                                                                                start=False, stop=True))
                        pe(mms, [K("qT", gp * 4 + rp * 2, tl), K("qT", gp * 4 + rp * 2 + 1, tl), K("kT", sl - 1), K("kT", sl), "c:ones", "c:fm_b"], (bkey,))
                        banks.append((bank, bkey))
                    for rp, (bank, bkey) in enumerate(banks):
                        op(DVE, lambda h, bank=bank, rp=rp: h.tensor_tensor(out=sbuf_s[:, rp * 2:rp * 2 + 2, :], in0=bank[:, :].rearrange("p (r k) -> p r k", r=2),
                                                                            in1=bias[:, 4 * g + rp * 2:4 * g + rp * 2 + 2, :], op=ALU.add),
                           (bkey, "c:bias"), (sk,))
                    op(DVE, lambda h: h.tensor_reduce(out=sa[:, 0:4], in_=sbuf_s[:, :, :], axis=AX.X, op=ALU.max), (sk,) + kk, kk)
                    op(DVE, lambda h: h.scalar_tensor_tensor(out=sa[:, 4:8], in0=sa[:, 0:4], scalar=-1.0, in1=nsink_bc[:, 4 * g:4 * g + 4],
                                                             op0=ALU.mult, op1=ALU.min), ("sink_bc",) + kk, kk)
                    op(DVE, lambda h: h.tensor_tensor(out=sa[:, 8:12], in0=sink_bc[:, 4 * g:4 * g + 4], in1=sa[:, 4:8], op=ALU.add), ("sink_bc",) + kk, kk)
                    return (tl, g, bi_)

                def stageA2e(ctx):
                    tl, g, bi_ = ctx
                    sbuf_s, sbuf_p, sbuf_pn = s_sb[bi_], p_sb[bi_], pn_sb[bi_]
                    sk, pk, pnk = f"s_sb{bi_}", f"p_sb{bi_}", f"pn_sb{bi_}"
                    sa = stA[:, bi_, :]
                    kk = (f"stA{bi_}",)
                    for r in range(4):
                        op(ACT, lambda h, r=r: h.activation(out=sbuf_p[:, r, :], in_=sbuf_s[:, r, :], func=AF.Exp, bias=sa[:, 4 + r:5 + r],
                                                             accum_out=sa[:, 12 + r:13 + r]), (sk,) + kk, (pk,) + kk)
                    op(ACT, lambda h: h.activation(out=sa[:, 8:12], in_=sa[:, 8:12], func=AF.Exp), kk, kk)

                def stageA2d(ctx):
                    tl, g, bi_ = ctx
                    sbuf_s, sbuf_p, sbuf_pn = s_sb[bi_], p_sb[bi_], pn_sb[bi_]
                    sk, pk, pnk = f"s_sb{bi_}", f"p_sb{bi_}", f"pn_sb{bi_}"
                    sa = stA[:, bi_, :]
                    kk = (f"stA{bi_}",)
                    op(DVE, lambda h: h.tensor_tensor(out=sa[:, 16:20], in0=sa[:, 12:16], in1=sa[:, 8:12], op=ALU.add), kk, kk)
                    op(DVE, lambda h: h.reciprocal(out=sa[:, 20:24], in_=sa[:, 16:20]), kk, kk)
                    op(DVE, lambda h: h.tensor_tensor(out=sbuf_pn[:, :, :], in0=sbuf_p[:, :, :],
                                                      in1=sa[:, 20:24].unsqueeze(2).broadcast_to([128, 4, 256]), op=ALU.mult),
                       (pk,) + kk, (pnk,))

                OT = {}

                def stageB1(ctx):
                    tl, g, bi_ = ctx
                    sbuf_pn, sbuf_pt = pn_sb[bi_], PT_sb[bi_]
                    pnk, ptk = f"pn_sb{bi_}", f"PT_sb{bi_}"
                    bankT, btkey = nb()
                    pe([lambda t, kb=kb, r=r: t.transpose(out=bankT[:, (kb * 4 + r) * 128:(kb * 4 + r + 1) * 128],
                                                            in_=sbuf_pn[:, r, kb * 128:(kb + 1) * 128], identity=ident[:])
                        for kb in range(2) for r in range(4)], (pnk, "c:ident"), (btkey,))
                    copy(ACT, sbuf_pt[:, :], bankT[:, :], (btkey,), (ptk,))

                def stageB2(ctx):
                    tl, g, bi_ = ctx
                    sl = tl + 1
                    tcols = slice(tl * 128, (tl + 1) * 128)
                    gp, half = g // 2, g % 2
                    rows = slice(half * 64, half * 64 + 64)
                    sbuf_pt = PT_sb[bi_]
                    ptk = f"PT_sb{bi_}"
                    if half == 0:
                        OT[(tl, gp)] = nf()
                        resv.add(int(OT[(tl, gp)][1][2:]))
                    obank, okey = OT[(tl, gp)]
                    pe([lambda t, kb=kb: t.matmul(obank[rows, :], lhsT=Vt[:, sl - 1 + kb, g * 64:(g + 1) * 64], rhs=sbuf_pt[:, kb * 512:(kb + 1) * 512],
                                                   start=(kb == 0), stop=(kb == 1), tile_position=(0, half * 64)) for kb in range(2)],
                       (ptk, K("V", sl - 1), K("V", sl)), (okey,))

                def stageB3(ctx):
                    tl, g, bi_ = ctx
                    tcols = slice(tl * 128, (tl + 1) * 128)
                    gp, half = g // 2, g % 2
                    if half == 1:
                        obank, okey = OT[(tl, gp)]
                        copy(ACT, ybT[:, gp * 4:gp * 4 + 4, tcols], obank[:, :].rearrange("p (r q) -> p r q", r=4), (okey,),
                             [K("ybT", gp * 4 + r, tl) for r in range(4)])
                        resv.discard(int(okey[2:]))

                def attn_gen():
                    units = [(tl, g) for tl in active if gt(tl) != SAMPLE_T for g in range(4)]
                    p1 = p2 = p3 = None
                    for i, u_ in enumerate(units + [None, None, None]):
                        ctx = None
                        if u_ is not None:
                            tl, g = u_
                            ctx = stageA(tl, g, i % 2)
                        if p1 is not None:
                            stageA2d(p1)
                        if p2 is not None:
                            stageB1(p2)
                        yield
                        if ctx is not None:
                            stageA2e(ctx)
                        if p3 is not None:
                            stageB2(p3)
                            stageB3(p3)
                        p3, p2, p1 = p2, p1, ctx
                        yield

                def dense_gen(phase):
                    key_ = ("ga_pa", "gb_pb")[phase]
                    src = uT if phase == 0 else ybT
                    sname = "uT" if phase == 0 else "ybT"
                    for bi, (gblk, pblk) in enumerate(pl[key_]):
                        sg, spj = ws.slot(gblk), ws.slot(pblk)
                        for cc in range(2):
                            c = bi * 2 + cc
                            for seg in segs:
                                n = len(seg) * 128
                                gb_ = (c * 2 + segs.index(seg)) % 2
                                bank, bkey = ws_job(f"ring{sg}", lambda kc, sg=sg, cc=cc: ring[:, sg, kc, cc * 128:(cc + 1) * 128],
                                                    lambda kc, seg=seg: xT[:, kc, cols(seg)], xkeys(seg), n)
                                op(ACT, lambda h, bank=bank, gb_=gb_, seg=seg, n=n: h.activation(out=gtmp[:, gb_, cols(seg)], in_=bank[:, 0:n], func=AF.Sigmoid),
                                   (bkey,), (f"gtmp{gb_}",))
                                bank2, bkey2 = ws_job(f"ring{spj}", lambda kc, spj=spj, cc=cc: ring[:, spj, kc, cc * 128:(cc + 1) * 128],
                                                      lambda kc, seg=seg: src[:, kc, cols(seg)],
                                                      [K(sname, kc, tl) for kc in range(8) for tl in seg], n)
                                mkeys = [K("mixT", c, tl) for tl in seg]
                                if phase == 0:
                                    op(DVE, lambda h, bank2=bank2, gb_=gb_, seg=seg, n=n, c=c: h.tensor_tensor(out=mixT[:, c, cols(seg)], in0=bank2[:, 0:n],
                                                                                                          in1=gtmp[:, gb_, cols(seg)], op=ALU.mult),
                                       (bkey2, f"gtmp{gb_}"), mkeys)
                                else:
                                    op(DVE, lambda h, bank2=bank2, gb_=gb_, seg=seg, n=n: h.tensor_tensor(out=ftmp[:, 0:n], in0=bank2[:, 0:n],
                                                                                                     in1=gtmp[:, gb_, cols(seg)], op=ALU.mult),
                                       (bkey2, f"gtmp{gb_}"), ("scr2k",))
                                    op(DVE, lambda h, seg=seg, n=n, c=c: h.tensor_tensor(out=mixT[:, c, cols(seg)], in0=ftmp[:, 0:n],
                                                                                       in1=mixT[:, c, cols(seg)], op=ALU.add),
                                       ["scr2k"] + mkeys, mkeys)
                                yield
                        ws.done(gblk)
                        ws.done(pblk)

                def pre_gen():
                    for bi, blk in list(enumerate(pl["u"]))[:2]:
                        s = ws.slot(blk)
                        for cc in range(2):
                            c = bi * 2 + cc
                            for seg in segs:
                                n = len(seg) * 128
                                bank, bkey = ws_job(f"ring{s}", lambda kc, s=s, cc=cc: ring[:, s, kc, cc * 128:(cc + 1) * 128],
                                                    lambda kc, seg=seg: xT[:, kc, cols(seg)], xkeys(seg), n)
                                op(ACT, lambda h, bank=bank, c=c, seg=seg, n=n: h.activation(out=uT[:, c, cols(seg)], in_=bank[:, 0:n], func=AF.Gelu_apprx_tanh),
                                   (bkey,), [K("uT", c, tl) for tl in seg])
                                yield
                        ws.done(blk)

                    def u_steps():
                        for bi, blk in list(enumerate(pl["u"]))[2:]:
                            s = ws.slot(blk)
                            for cc in range(2):
                                c = bi * 2 + cc
                                for seg in segs:
                                    n = len(seg) * 128
                                    bank, bkey = ws_job(f"ring{s}", lambda kc, s=s, cc=cc: ring[:, s, kc, cc * 128:(cc + 1) * 128],
                                                        lambda kc, seg=seg: xT[:, kc, cols(seg)], xkeys(seg), n)
                                    op(ACT, lambda h, bank=bank, c=c, seg=seg, n=n: h.activation(out=uT[:, c, cols(seg)], in_=bank[:, 0:n], func=AF.Gelu_apprx_tanh),
                                       (bkey,), [K("uT", c, tl) for tl in seg])
                                    yield


                    load_ln(l, 0)
                    vs_ = [ws.slot(b_) for b_ in pl["v"]]
                    ug_ = u_steps()
                    u_alive = True
                    for tl in active:
                        for hf in range(2):
                            bank, bkey = nf()
                            pe([lambda t, kc=kc, q2=q2, tl=tl: t.matmul(bank[:, q2 * BW:(q2 + 1) * BW], lhsT=xT[:, kc, tl * 128:(tl + 1) * 128],
                                                                          rhs=ring[:, vs_[hf * 2 + q2], kc, :], start=(kc == 0), stop=(kc == 7))
                                for q2 in range(2) for kc in range(8)],
                               [f"ring{vs_[hf * 2]}", f"ring{vs_[hf * 2 + 1]}", K("xT", tl)], (bkey,))
                            op(ACT, lambda h, bank=bank, hf=hf: h.activation(out=tmpv[:, hf * 512:(hf + 1) * 512], in_=bank[:, :], func=AF.Gelu_apprx_tanh),
                               (bkey,), ("tmpv",))
                        if gt(tl) == SAMPLE_T:
                            layer_norm(tmpv[:], "tmpv", LN_EPS)
                            copy(DVE, va[:, tl, :], tmpv[:], ("tmpv",), [K("va", tl)])
                        else:
                            layer_norm(tmpv[:], "tmpv", LN_EPS, out2=va[:, tl, :], out2_keys=[K("va", tl)])
                        if gt(tl) == SAMPLE_T and not os.environ.get('NO_GV'):
                            [dma(OQ, gv[l, :, q_ * 256:(q_ + 1) * 256], tmpv[:, q_ * 256:(q_ + 1) * 256], reads=("tmpv",), is_out=True) for q_ in range(4)]
                        yield
                        if u_alive:
                            try:
                                next(ug_)
                                yield
                            except StopIteration:
                                u_alive = False
                    for _ in ug_:
                        yield
                    for b_ in pl["v"]:
                        ws.done(b_)
                    for b_ in pl["u"][2:]:
                        ws.done(b_)

                    for tl in active:
                        smp = gt(tl) == SAMPLE_T
                        wsb = wsb_s if smp else wsb_p
                        wk = "wsb_s" if smp else "wsb_p"
                        bi_ = 1 if smp else 0
                        for hf in range(2):
                            bank, bkey = nf()
                            mms = []
                            for gg in range(4):
                                gi = hf * 4 + gg
                                mms.append(lambda t, gi=gi, gg=gg, tl=tl: t.matmul(bank[:, gg * 128:(gg + 1) * 128], lhsT=va[:, tl, gi * 128:(gi + 1) * 128],
                                                                                  rhs=wsb[:, gi, :], start=True, stop=False))
                                mms.append(lambda t, gi=gi, gg=gg: t.matmul(bank[:, gg * 128:(gg + 1) * 128], lhsT=ones_b[0:33, :],
                                                                           rhs=bsrow[0:33, bi_, gi * 128:(gi + 1) * 128], start=False, stop=True))
                            pe(mms, [K("va", tl), wk, "bsrow", "c:ones"], (bkey,))
                            ukeys = [K("uT", hf * 4 + gg, tl) for gg in range(4)]
                            op(DVE, lambda h, bank=bank, hf=hf, tl=tl: h.tensor_tensor(out=uT[:, hf * 4:hf * 4 + 4, tl * 128:(tl + 1) * 128],
                                                                                       in0=bank[:, :].rearrange("p (g t) -> p g t", g=4),
                                                                                       in1=uT[:, hf * 4:hf * 4 + 4, tl * 128:(tl + 1) * 128], op=ALU.mult),
                               [bkey] + ukeys, ukeys)
                            yield

                    yield from dense_gen(0)

                ag_, pg_ = attn_gen(), pre_gen()
                a_alive = p_alive = True
                while a_alive or p_alive:
                    if a_alive:
                        try:
                            next(ag_)
                        except StopIteration:
                            a_alive = False
                    for _ in range(1):
                        if p_alive:
                            try:
                                next(pg_)
                            except StopIteration:
                                p_alive = False

                if has_sample and not os.environ.get('SKIP_SATT'):
                    evs_ = [e for e in (last_ev(PE), last_ev(ACT), last_ev(DVE)) if e]
                    for eng_ in (PE, ACT, DVE):
                        trk.wait(eng_, trk.deps((), (), evs_))
                    tl = GT - 1
                    sl = tl + 1
                    scol0 = tl * 128
                    OTs = [nf(), nf()]
                    for _, k_ in OTs:
                        resv.add(int(k_[2:]))
                    cache_ev = [last_ev(PE), last_ev(ACT), last_ev(DVE)]

                    def ots(b, gp):
                        bank, _ = OTs[b // 8]
                        o = ((b % 8) * 2 + gp) * 32
                        return bank[:, o:o + 32]
                    okeys = [OTs[0][1], OTs[1][1]]
                    op(DVE, lambda h: h.memset(pnew_all[:, :], 0.0), (), ("pnew_all",))
                    for gp in range(2):
                        copy(DVE, qs_sb[:, :, gp, :].rearrange("p b (r i) -> p b r i", r=4),
                             qT[:, gp * 4:gp * 4 + 4, scol0:scol0 + 128].rearrange("p r (b i) -> p b r i", b=16),
                             [K("qT", gp * 4 + r, tl) for r in range(4)], ("qs_sb",))
                    for rb in range(4):
                        bb = rb % 2
                        dma(plq, Kc_bf[bb], ck[l, rb * 4:rb * 4 + 4].rearrange("b p f -> p b f"), writes=(f"Kc{bb}",), extra=[e for e in cache_ev if e])
                        dma(plq, Vc_bf[bb], cv[l, rb * 4:rb * 4 + 4].rearrange("b p f -> p b f"), writes=(f"Vc{bb}",), extra=[e for e in cache_ev if e])
                        bankT, btkey = nb()
                        pe([lambda t, bl=bl, gp=gp: t.transpose(out=bankT[:, (bl * 2 + gp) * 128:(bl * 2 + gp + 1) * 128],
                                                                  in_=Kc_bf[bb][:, bl, gp * 128:(gp + 1) * 128], identity=ident[:])
                            for bl in range(4) for gp in range(2)], (f"Kc{bb}", "c:ident"), (btkey,))
                        copy(ACT, KcT[:, :, :, :], bankT[:, :].rearrange("p (b g k) -> p b g k", b=4, g=2), (btkey,), ("KcT",))
                        sc_bank, sc_key = nf()
                        sn_bank, sn_key = nf()
                        mms = []
                        for bl in range(4):
                            b = rb * 4 + bl
                            for g in range(4):
                                gp, half = g // 2, g % 2
                                rows = slice(half * 64, half * 64 + 64)
                                mms.append(lambda t, bl=bl, b=b, g=g, gp=gp, half=half, rows=rows: t.matmul(
                                    sc_bank[32 * g:32 * g + 32, bl * 128:(bl + 1) * 128],
                                    lhsT=qs_sb[rows, b, gp, :],
                                    rhs=KcT[rows, bl, gp, :], start=True, stop=True, tile_position=(half * 64, 32 * g)))
                                mms.append(lambda t, bl=bl, b=b, g=g, gp=gp, half=half, rows=rows: t.matmul(
                                    sn_bank[32 * g:32 * g + 32, bl * 8:(bl + 1) * 8],
                                    lhsT=qs_sb[rows, b, gp, :],
                                    rhs=kT[rows, gp, sl * 128 + b * 8:sl * 128 + b * 8 + 8], start=True, stop=True, tile_position=(half * 64, 32 * g)))
                        pe(mms, ["KcT", K("kT", sl), "qs_sb"], (sc_key, sn_key))
                        op(DVE, lambda h: h.tensor_tensor(out=ss_sb[:, :, 0:128], in0=sc_bank[:, :].rearrange("p (b k) -> p b k", b=4),
                                                          in1=bias_s[:, :, 0:128], op=ALU.add), (sc_key, "c:bias_s"), ("ss_sb",))
                        op(DVE, lambda h: h.tensor_tensor(out=ss_sb[:, :, 128:136], in0=sn_bank[:, 0:32].rearrange("p (b k) -> p b k", b=4),
                                                          in1=bias_s[:, :, 128:136], op=ALU.add), (sn_key, "c:bias_s"), ("ss_sb",))
                        kk = ("st",)
                        op(DVE, lambda h: h.tensor_reduce(out=st[:, 16:20], in_=ss_sb[:, :, :], axis=AX.X, op=ALU.max), ("ss_sb",) + kk, kk)
                        op(DVE, lambda h: h.tensor_tensor(out=st[:, 16:20], in0=st[:, 16:20], in1=sink_s4[:, :], op=ALU.max), ("sink_s4",) + kk, kk)
                        op(DVE, lambda h: h.tensor_scalar(out=st[:, 20:24], in0=st[:, 16:20], scalar1=-1.0, scalar2=None, op0=ALU.mult), kk, kk)
                        op(DVE, lambda h: h.tensor_tensor(out=st[:, 24:28], in0=sink_s4[:, :], in1=st[:, 20:24], op=ALU.add), ("sink_s4",) + kk, kk)
                        for bl in range(4):
                            op(ACT, lambda h, bl=bl: h.activation(out=ps_sb[:, bl, :], in_=ss_sb[:, bl, :], func=AF.Exp, bias=st[:, 20 + bl:21 + bl],
                                                                   accum_out=st[:, 28 + bl:29 + bl]), ("ss_sb",) + kk, ("ps_sb",) + kk)
                        op(ACT, lambda h: h.activation(out=st[:, 24:28], in_=st[:, 24:28], func=AF.Exp), kk, kk)
                        op(DVE, lambda h: h.tensor_tensor(out=st[:, 32:36], in0=st[:, 28:32], in1=st[:, 24:28], op=ALU.add), kk, kk)
                        op(DVE, lambda h: h.reciprocal(out=st[:, 36:40], in_=st[:, 32:36]), kk, kk)
                        for bl in range(4):
                            op(DVE, lambda h, bl=bl: h.tensor_scalar(out=pns_sb[:, bl, :], in0=ps_sb[:, bl, :], scalar1=st[:, 36 + bl:37 + bl], scalar2=None, op0=ALU.mult),
                               ("ps_sb",) + kk, ("pns_sb",))
                        copy(DVE, pnew_all[:, rb * 32:(rb + 1) * 32].rearrange("p (b k) -> p b k", b=4), pns_sb[:, :, 128:136], ("pns_sb",), ("pnew_all",))
                        bankT2, bt2key = nb()
                        pe([lambda t, bl=bl: t.transpose(out=bankT2[:, bl * 128:(bl + 1) * 128], in_=pns_sb[:, bl, 0:128], identity=ident[:])
                            for bl in range(4)], ("pns_sb", "c:ident"), (bt2key,))
                        copy(ACT, PTc_sb[:, :], bankT2[:, 0:512], (bt2key,), ("PTc_sb",))
                        bankT3, bt3key = nb()
                        pe([lambda t: t.transpose(out=bankT3[:, 0:128], in_=pnew_all[:, :], identity=ident[:])], ("pnew_all", "c:ident"), (bt3key,))
                        copy(ACT, PTn_sb[:, :], bankT3[:, 0:128], (bt3key,), ("PTn_sb",))
                        for bl in range(4):
                            b = rb * 4 + bl
                            op(DVE, lambda h, b=b, bl=bl: h.tensor_scalar(out=PTn_bd[:, bl, :], in0=PTn_sb[:, :], scalar1=bmask[:, b:b + 1], scalar2=None, op0=ALU.mult),
                               ("PTn_sb", "c:bmask"), ("PTn_bd",))
                        mms = []
                        for bl in range(4):
                            b = rb * 4 + bl
                            for g in range(4):
                                gp, half = g // 2, g % 2
                                mms.append(lambda t, bl=bl, b=b, g=g, gp=gp, half=half: t.matmul(
                                    ots(b, gp)[half * 64:half * 64 + 64, :], lhsT=Vc_bf[bb][:, bl, g * 64:(g + 1) * 64],
                                    rhs=PTc_sb[:, bl * 128 + g * 32:bl * 128 + g * 32 + 32], start=True, stop=False, tile_position=(0, half * 64)))
                                mms.append(lambda t, bl=bl, b=b, g=g, gp=gp, half=half: t.matmul(
                                    ots(b, gp)[half * 64:half * 64 + 64, :], lhsT=Vt[:, sl, g * 64:(g + 1) * 64],
                                    rhs=PTn_bd[:, bl, g * 32:g * 32 + 32], start=False, stop=True, tile_position=(0, half * 64)))
                        pe(mms, ("PTc_sb", f"Vc{bb}", "PTn_bd", K("V", sl)), okeys)
                    for hb in range(2):
                        for gp in range(2):
                            bank, bkey = OTs[hb]
                            src = bank[:, :].rearrange("p (b g r i) -> p g r b i", b=8, g=2, r=4)[:, gp]
                            dst = ybT[:, gp * 4:gp * 4 + 4, scol0 + hb * 64:scol0 + hb * 64 + 64].rearrange("p r (b i) -> p r b i", b=8)
                            copy(DVE, dst, src, (bkey,), [K("ybT", gp * 4 + r, tl) for r in range(4)])

                resv.clear()
                copy(DVE, carK[:, l, :, :], kT[:, :, GT * 128:(GT + 1) * 128], [K("kT", GT)], ("carK",))
                copy(DVE, carV[:, l, :], Vt[:, GT, :], [K("V", GT)], ("carV",))

                for _ in dense_gen(1):
                    pass

                r2_ev = [e for e in (last_ev(PE), last_ev(ACT), last_ev(DVE)) if e]
                wdv = w_down[l].rearrange("(kc p) n -> p kc n", p=128)
                dma(plq, wdn[:, 0:11, :], wdv[:, 0:11, :], writes=("wdn",), extra=r2_ev)
                dma(plq, wdn[:, 11:22, :], wdv[:, 11:22, :], writes=("wdn",), extra=r2_ev)

                load_ln(l, 1)
                if G == groups[0] and l == 0:
                    for l_ in range(depth_run):
                        dma(spq, ks[l_, :, 0:120, :], ck[l_, :, 8:128, :], is_out=True)
                        dma(spq, vs[l_, :, 0:120, :], cv[l_, :, 8:128, :], is_out=True)
                wos_ = [ws.slot(b_) for b_ in pl["wo"]]
                for tl in active:
                    for hf in range(2):
                        bank, bkey = nf()
                        pe([lambda t, kc=kc, q2=q2, tl=tl: t.matmul(bank[:, q2 * BW:(q2 + 1) * BW], lhsT=mixT[:, kc, tl * 128:(tl + 1) * 128],
                                                                      rhs=ring[:, wos_[hf * 2 + q2], kc, :], start=(kc == 0), stop=(kc == 7))
                            for q2 in range(2) for kc in range(8)],
                           [f"ring{wos_[hf * 2]}", f"ring{wos_[hf * 2 + 1]}"] + [K("mixT", kc, tl) for kc in range(8)], (bkey,))
                        op(DVE, lambda h, bank=bank, hf=hf, tl=tl: h.scalar_tensor_tensor(out=x_tok[:, tl, hf * 512:(hf + 1) * 512], in0=bank[:, :], scalar=c_res,
                                                                                         in1=x_tok[:, tl, hf * 512:(hf + 1) * 512], op0=ALU.mult, op1=ALU.add),
                           (bkey, xk(tl)), (xk(tl),))
                    ln_stats(tl)
                for b_ in pl["wo"]:
                    ws.done(b_)
                ln_group(active, eps_res, want_xT=True)

                for bi, (gblk, ublk) in enumerate(pl["gu"]):
                    sg, su = ws.slot(gblk), ws.slot(ublk)
                    for cc in range(2):
                        c = bi * 2 + cc
                        for seg in segs:
                            n = len(seg) * 128
                            gb_ = (c * 2 + segs.index(seg)) % 2
                            bank, bkey = ws_job(f"ring{sg}", lambda kc, sg=sg, cc=cc: ring[:, sg, kc, cc * 128:(cc + 1) * 128],
                                                lambda kc, seg=seg: xT[:, kc, cols(seg)], xkeys(seg), n)
                            op(ACT, lambda h, bank=bank, gb_=gb_, seg=seg, n=n: h.activation(out=gtmp[:, gb_, cols(seg)], in_=bank[:, 0:n], func=AF.Silu),
                               (bkey,), (f"gtmp{gb_}",))
                            bank2, bkey2 = ws_job(f"ring{su}", lambda kc, su=su, cc=cc: ring[:, su, kc, cc * 128:(cc + 1) * 128],
                                                  lambda kc, seg=seg: xT[:, kc, cols(seg)], xkeys(seg), n)
                            op(DVE, lambda h, bank2=bank2, gb_=gb_, seg=seg, n=n, c=c: h.tensor_tensor(out=hT[:, c, cols(seg)], in0=bank2[:, 0:n],
                                                                                                  in1=gtmp[:, gb_, cols(seg)], op=ALU.mult),
                               (bkey2, f"gtmp{gb_}"), [K("hT", c, tl) for tl in seg])
                    ws.done(gblk)
                    ws.done(ublk)

                load_ln(l, 2)
                for tl in active:
                    for hf in range(2):
                        bank, bkey = nf()
                        pe([lambda t, kc=kc, tl=tl, hf=hf: t.matmul(bank[:, :], lhsT=hT[:, kc, tl * 128:(tl + 1) * 128], rhs=wdn[:, kc, hf * 512:(hf + 1) * 512],
                                                                      start=(kc == 0), stop=(kc == NHC - 1)) for kc in range(NHC)],
                           ["wdn"] + [K("hT", kc, tl) for kc in range(NHC)], (bkey,))
                        op(DVE, lambda h, bank=bank, hf=hf, tl=tl: h.scalar_tensor_tensor(out=x_tok[:, tl, hf * 512:(hf + 1) * 512], in0=bank[:, :], scalar=c_res,
                                                                                         in1=x_tok[:, tl, hf * 512:(hf + 1) * 512], op0=ALU.mult, op1=ALU.add),
                           (bkey, xk(tl)), (xk(tl),))
                    ln_stats(tl)
                ln_group(active, eps_res, want_xT=(l < depth_run - 1))
                for tl in active:
                    if l < depth_run - 1:
                        pass
                    else:
                        t_ = gt(tl)
                        if t_ == SAMPLE_T:
                            dma(spq, ys, x_tok[:, tl, :], reads=(xk(tl),), is_out=True)
                        elif t_ >= 4:
                            dma(spq, yp[(t_ - 4) * 128:(t_ - 3) * 128, :], x_tok[:, tl, :], reads=(xk(tl),), is_out=True)


        flush_deferred()
        done = {}
        for (s_, v_, k_) in out_events:
            if done.get(k_, (None, 0))[1] < v_:
                done[k_] = (s_, v_)
        for k_, (s_, v_) in done.items():
            nc.sync.wait_ge(s_, v_)
    return nc


def _rel_bucket(dist):
    n = np.maximum(dist, 0)
    max_exact = 16
    nf = np.maximum(n, 1).astype(np.float32)
    large = max_exact + (np.log(nf / np.float32(max_exact)) / np.float32(math.log(128 / max_exact)) * np.float32(32 - max_exact)).astype(np.int32)
    large = np.minimum(large, 31)
    return np.where(n < max_exact, n, large)


_NC_CACHE = {}


def _prep(x_prompt, x_sample, cache_swa_k, cache_swa_v, rel_bias, w_in, ln_v_g, ln_v_b, w_s, b_s, sinks,
          w_pa, w_pb, w_o, ln1_g, ln1_b, w_gate, w_up, w_down, ln2_g, ln2_b):
    f32 = np.float32
    A = lambda a: np.ascontiguousarray(np.asarray(a, dtype=f32))
    x_prompt, x_sample = A(x_prompt), A(x_sample)
    ck_all = A(cache_swa_k).reshape(DEPTH, 128, 128, 256)
    cv_all = A(cache_swa_v).reshape(DEPTH, 128, 128, 256)
    w_in, w_pb = A(w_in), A(w_pb)
    perm = np.zeros(1024, dtype=np.int64)
    for gp in range(2):
        for r in range(4):
            for half in range(2):
                h = 4 * (2 * gp + half) + r
                n0 = (gp * 4 + r) * 128 + half * 64
                perm[n0:n0 + 64] = h * 64 + np.arange(64)
    w_in_p = w_in.copy()
    w_in_p[:, :, O_Q:O_Q + 1024] = w_in[:, :, O_Q + perm]
    w_pb_p = np.ascontiguousarray(w_pb[:, perm, :])
    lnp = np.stack([A(ln_v_g), A(ln_v_b), A(ln1_g), A(ln1_b), A(ln2_g), A(ln2_b)], axis=1)
    w_s = A(w_s)
    wsT_p = np.ascontiguousarray(w_s.transpose(0, 3, 1, 2))
    w8 = w_s[:, :, :8, :8]
    wsT_s = np.ascontiguousarray(np.tile(w8.transpose(0, 3, 1, 2), (1, 16, 1, 16)))
    b_s = A(b_s)
    bs_p = np.ascontiguousarray(b_s.reshape(DEPTH, 1, 1024))
    bs_s = np.ascontiguousarray(np.tile(b_s[:, :, :8], (1, 1, 16)).reshape(DEPTH, 1, 1024))
    sinks = A(sinks)
    sinks_s4 = np.ascontiguousarray(np.repeat(np.repeat(sinks, 8, axis=1)[:, :, None], 4, axis=2))
    rel_bias = A(rel_bias)
    qi = np.arange(128)[:, None]
    kj = np.arange(256)[None, :]
    dist = qi + 128 - kj
    bias_val = np.ascontiguousarray(rel_bias[_rel_bucket(dist)].transpose(0, 2, 1))
    m2 = np.where((dist >= 0) & (dist < 128), 0.0, NEG).astype(f32)
    mask_p = np.ascontiguousarray(np.broadcast_to(m2[:, None, :], (128, 16, 256)))
    qs = np.arange(8)[:, None]
    ksj = np.arange(136)[None, :]
    dist_s = qs + 128 - ksj
    bsv = rel_bias[_rel_bucket(dist_s)]
    bsv = bsv.transpose(2, 0, 1).reshape(128, 136)
    bias_s_val = np.ascontiguousarray(np.broadcast_to(bsv[:, None, :], (128, 4, 136)))
    ms = np.where((dist_s >= 0) & (dist_s < 128), 0.0, NEG).astype(f32)
    ms = np.tile(ms, (16, 1))
    mask_s = np.ascontiguousarray(np.broadcast_to(ms[:, None, :], (128, 4, 136)))
    cst = np.zeros((128, 3, 128), dtype=f32)
    cst[:, 0, :] = np.eye(128, dtype=f32)
    cst[:, 1, :] = (np.arange(128)[:, None] <= np.arange(128)[None, :]).astype(f32)
    sidx = np.arange(128)
    cst[:, 2, :] = ((sidx[:, None] // 8 == sidx[None, :] // 8) & (sidx[:, None] % 8 <= sidx[None, :] % 8)).astype(f32)
    bmask = (sidx[:, None] // 8 == np.arange(16)[None, :]).astype(f32)

    shared = dict(w_in=w_in_p, w_pa=A(w_pa), w_pb=w_pb_p, w_o=A(w_o), w_gate=A(w_gate), w_up=A(w_up), w_down=A(w_down),
                  lnp=lnp, wsT_p=wsT_p, wsT_s=wsT_s, bs_p=bs_p, bs_s=bs_s, sinks=sinks, sinks_s4=sinks_s4,
                  bias_val=bias_val, mask_p=mask_p, bias_s_val=bias_s_val, mask_s=mask_s, cst=cst, bmask=bmask)
    in_maps = []
    for c in range(NCORE):
        bq, half = c // 2, c % 2
        xin = np.zeros((NT * 128, D), dtype=f32)
        if half == 1:
            xin[0:512] = x_prompt[bq, 2048 - 512:2048]
        xin[512:512 + 2048] = x_prompt[bq, half * 2048:(half + 1) * 2048]
        xin[2560:2688] = x_sample[c * 16:(c + 1) * 16].reshape(128, D)
        fm = np.zeros((1, 256), dtype=f32)
        if half == 0:
            fm[0, :128] = NEG
        m = dict(shared)
        m.update(xin=xin, ck=np.ascontiguousarray(ck_all[:, c * 16:(c + 1) * 16]), cv=np.ascontiguousarray(cv_all[:, c * 16:(c + 1) * 16]), fmask=fm)
        in_maps.append(m)
    return in_maps


def kernel(**inputs):
    f32 = np.float32
    in_maps = _prep(**inputs)
    if "nc" not in _NC_CACHE:
        _NC_CACHE["nc"] = build_program()
    nc = _NC_CACHE["nc"]
    res = run_bass_kernel_spmd(nc, in_maps, core_ids=list(range(NCORE)))
    R = res.results
    y_prompt = np.zeros((4, 4096, D), dtype=f32)
    y_sample = np.zeros((128, 8, D), dtype=f32)
    kp = np.zeros((DEPTH, 4, 128, 4, 64), dtype=f32)
    vp = np.zeros((DEPTH, 4, 128, 4, 64), dtype=f32)
    ks = np.zeros((DEPTH, 128, 128, 4, 64), dtype=f32)
    vs = np.zeros((DEPTH, 128, 128, 4, 64), dtype=f32)
    gvo = np.zeros((DEPTH, 128, 8, D), dtype=f32)
    for c in range(NCORE):
        bq, half = c // 2, c % 2
        r = R[c]
        y_prompt[bq, half * 2048:(half + 1) * 2048] = np.asarray(r["yp"])
        y_sample[c * 16:(c + 1) * 16] = np.asarray(r["ys"]).reshape(16, 8, D)
        if half == 1:
            kp[:, bq] = np.asarray(r["kp"]).reshape(DEPTH, 128, 4, 64)
            vp[:, bq] = np.asarray(r["vp"]).reshape(DEPTH, 128, 4, 64)
        ks[:, c * 16:(c + 1) * 16] = np.asarray(r["ks"]).reshape(DEPTH, 16, 128, 4, 64)
        vs[:, c * 16:(c + 1) * 16] = np.asarray(r["vs"]).reshape(DEPTH, 16, 128, 4, 64)
        gvo[:, c * 16:(c + 1) * 16] = np.asarray(r["gv"]).reshape(DEPTH, 16, 8, D)
    return (y_prompt, y_sample, kp, vp, ks, vs, gvo)
```
